# Optimizing a Trainium2 kernel written in Bass

```python
import jax, jax.numpy as jnp
from jax import lax
import numpy as np

D_MODEL = 1024
BATCH = 8
SEQ = 4096
DEPTH = 4

N_MIXERS = 2
FOURIER_GROUPS = 4
FOURIER_GROUP_DIM = D_MODEL // FOURIER_GROUPS
RET_HEADS = 4
RET_QK_DIM = D_MODEL // RET_HEADS
RET_V_DIM = 2 * RET_QK_DIM
RET_CHUNK = 128
ROPE_BASE = 10000.0
D_FF = -(-8 * D_MODEL // (3 * 256)) * 256
EPS = 1e-6

kernel_name = "hybrid_fourier_retention_adaln_encoder"


def rmsnorm(x, g):
    xf = x.astype(jnp.float32)
    y = xf * lax.rsqrt(jnp.mean(xf * xf, axis=-1, keepdims=True) + EPS)
    return (y * g.astype(jnp.float32)).astype(x.dtype)


def modulate(h, shift, scale):
    return h * (1.0 + scale[:, None, :]) + shift[:, None, :]


def fourier_mixer(h, w_o):
    B, S, D = h.shape
    hg = h.astype(jnp.float32).reshape(B, S, FOURIER_GROUPS, FOURIER_GROUP_DIM)
    f = jnp.fft.fftn(hg, axes=(1, 3), norm="ortho").real.astype(h.dtype)
    return f.reshape(B, S, D) @ w_o


def rotary(x, pos):
    half = x.shape[-1] // 2
    inv_freq = ROPE_BASE ** (-jnp.arange(half, dtype=jnp.float32) / half)
    ang = pos[:, None] * inv_freq[None, :]
    cos, sin = jnp.cos(ang), jnp.sin(ang)
    x1, x2 = x[..., :half], x[..., half:]
    return jnp.concatenate([x1 * cos - x2 * sin, x1 * sin + x2 * cos], axis=-1)


def chunk_retention(q, k, v, log_gamma, inclusive):
    B, H, S, dk = q.shape
    dv = v.shape[-1]
    C = RET_CHUNK
    N = S // C
    to_chunks = lambda a: jnp.moveaxis(a.reshape(B, H, N, C, a.shape[-1]), 2, 0)
    qc, kc, vc = to_chunks(q), to_chunks(k), to_chunks(v)

    t = jnp.arange(C)
    diff = t[:, None] - t[None, :]
    mask = (diff >= 0) if inclusive else (diff > 0)
    lg = log_gamma.astype(jnp.float32)[:, None, None]
    dmask = jnp.where(mask, jnp.exp(jnp.where(mask, diff, 0).astype(jnp.float32) * lg), 0.0)
    tf = t.astype(jnp.float32)[None, :, None]
    q_decay = jnp.exp((tf + 1.0) * lg)
    k_decay = jnp.exp((C - 1.0 - tf) * lg)
    chunk_decay = jnp.exp(C * lg)

    def step(R, xs):
        qb, kb, vb = xs
        scores = jnp.einsum('bhtd,bhsd->bhts', qb, kb) * dmask
        inner = jnp.einsum('bhts,bhsv->bhtv', scores, vb)
        cross = jnp.einsum('bhtd,bhdv->bhtv', qb, R) * q_decay
        R = R * chunk_decay + jnp.einsum('bhsd,bhsv->bhdv', kb * k_decay, vb)
        return R, inner + cross

    R0 = jnp.zeros((B, H, dk, dv), jnp.float32)
    _, out = lax.scan(step, R0, (qc, kc, vc))
    return jnp.moveaxis(out, 0, 2).reshape(B, H, S, dv)


def retention_mixer(h, w_in, w_out, log_g_fwd, log_g_bwd):
    B, S, _ = h.shape
    H, dk, dv = RET_HEADS, RET_QK_DIM, RET_V_DIM
    proj = h @ w_in
    q, k, v, g = jnp.split(proj, [H * dk, 2 * H * dk, 2 * H * dk + H * dv], axis=-1)
    heads = lambda a, d: a.astype(jnp.float32).reshape(B, S, H, d).transpose(0, 2, 1, 3)
    pos = jnp.arange(S, dtype=jnp.float32)
    q = rotary(heads(q, dk), pos)
    k = rotary(heads(k, dk), pos) * (dk ** -0.5)
    v = heads(v, dv)
    y_fwd = chunk_retention(q, k, v, log_g_fwd, True)
    flip = lambda a: jnp.flip(a, axis=2)
    y_bwd = flip(chunk_retention(flip(q), flip(k), flip(v), log_g_bwd, False))
    y = y_fwd + y_bwd
    mu = jnp.mean(y, axis=-1, keepdims=True)
    var = jnp.mean(jnp.square(y - mu), axis=-1, keepdims=True)
    y = (y - mu) * lax.rsqrt(var + EPS)
    y = y.transpose(0, 2, 1, 3).reshape(B, S, H * dv).astype(h.dtype)
    return (jax.nn.silu(g) * y) @ w_out


def swiglu(h, w_in, w_out):
    gate, up = jnp.split(h @ w_in, 2, axis=-1)
    return (jax.nn.silu(gate) * up) @ w_out


def setup_inputs(seed: int = 0) -> dict:
    key = jax.random.key(seed)
    ks = jax.random.split(key, 14)
    D, F = D_MODEL, D_FF
    n_four = len(range(0, DEPTH, N_MIXERS))
    n_ret = len(range(1, DEPTH, N_MIXERS))
    ret_in_w = 2 * RET_HEADS * RET_QK_DIM + 2 * RET_HEADS * RET_V_DIM
    ret_v_w = RET_HEADS * RET_V_DIM
    nrm = lambda k, shape, s: jax.random.normal(k, shape, jnp.float32) * s
    return {
        "x": nrm(ks[0], (BATCH, SEQ, D), 1.0),
        "c": nrm(ks[1], (BATCH, D), 1.0),
        "w_ada": nrm(ks[2], (DEPTH, D, 6 * D), 0.5 * D ** -0.5),
        "b_ada": nrm(ks[3], (DEPTH, 6 * D), 0.01),
        "norm_mix_g": 1.0 + nrm(ks[4], (DEPTH, D), 0.02),
        "norm_ffn_g": 1.0 + nrm(ks[5], (DEPTH, D), 0.02),
        "w_fourier_out": nrm(ks[6], (n_four, D, D), D ** -0.5),
        "w_ret_in": nrm(ks[7], (n_ret, D, ret_in_w), D ** -0.5),
        "w_ret_out": nrm(ks[8], (n_ret, ret_v_w, D), ret_v_w ** -0.5),
        "w_ffn_in": nrm(ks[9], (DEPTH, D, 2 * F), D ** -0.5),
        "w_ffn_out": nrm(ks[10], (DEPTH, F, D), F ** -0.5),
        "final_norm_g": 1.0 + nrm(ks[11], (D,), 0.02),
        "w_ada_final": nrm(ks[12], (D, 2 * D), 0.5 * D ** -0.5),
        "b_ada_final": nrm(ks[13], (2 * D,), 0.01),
    }


def reference(x, c, w_ada, b_ada, norm_mix_g, norm_ffn_g, w_fourier_out, w_ret_in, w_ret_out,
              w_ffn_in, w_ffn_out, final_norm_g, w_ada_final, b_ada_final):
    hidx = jnp.arange(RET_HEADS, dtype=jnp.float32)
    log_g_fwd = jnp.log1p(-jnp.exp2(-5.0 - hidx))
    log_g_bwd = jnp.flip(log_g_fwd)
    c_act = jax.nn.silu(c)
    for i in range(DEPTH):
        mod = c_act @ w_ada[i] + b_ada[i]
        sh1, sc1, g1, sh2, sc2, g2 = jnp.split(mod, 6, axis=-1)
        h = modulate(rmsnorm(x, norm_mix_g[i]), sh1, sc1)
        j = i // N_MIXERS
        if i % N_MIXERS == 0:
            m = fourier_mixer(h, w_fourier_out[j])
        else:
            m = retention_mixer(h, w_ret_in[j], w_ret_out[j], log_g_fwd, log_g_bwd)
        x = x + g1[:, None, :] * m
        h = modulate(rmsnorm(x, norm_ffn_g[i]), sh2, sc2)
        x = x + g2[:, None, :] * swiglu(h, w_ffn_in[i], w_ffn_out[i])
    shf, scf = jnp.split(c_act @ w_ada_final + b_ada_final, 2, axis=-1)
    return modulate(rmsnorm(x, final_norm_g), shf, scf)
```

```python
import contextlib
import numpy as np
import ml_dtypes
import concourse.bass as bass
import concourse.mybir as mybir
from concourse.bass_utils import run_bass_kernel_spmd

F32 = mybir.dt.float32
BF16 = mybir.dt.bfloat16
ALU = mybir.AluOpType
AF = mybir.ActivationFunctionType
ENGS = ("pe", "act", "dve", "pool", "sp")

S, D, FF, NL = 4096, 1024, 2816, 4
H, DK, DV, CH = 4, 256, 512, 128
NCH = S // CH
EPS = 1e-6


class Buf:
    __slots__ = ("name", "ws", "rd", "prd")

    def __init__(self, name=""):
        self.name = name
        self.ws = []
        self.rd = []
        self.prd = []


class Op:
    __slots__ = ("eng", "fn", "reads", "writes", "pwrites", "dma", "deps", "sig", "sigval", "dsem",
                 "dval", "dprev", "eidx", "barrier")

    def __init__(self, eng, fn, reads, writes, pwrites, dma, barrier=False):
        self.eng = eng
        self.fn = fn
        self.reads = reads
        self.writes = writes
        self.pwrites = pwrites
        self.dma = dma
        self.barrier = barrier
        self.deps = ()
        self.sig = False
        self.sigval = 0
        self.dsem = None
        self.dval = 0
        self.dprev = 0
        self.eidx = 0


class Prog:
    def __init__(self, nc, n_dma_sems=8):
        self.nc = nc
        self.ops = []
        self.n_dma_sems = n_dma_sems

    def op(self, eng, fn, reads=(), writes=(), pwrites=(), dma=False):
        self.ops.append(Op(eng, fn, tuple(reads), tuple(writes), tuple(pwrites), dma))

    def dma(self, queue, out, in_, reads=(), writes=(), pwrites=()):
        self.op(queue, lambda e: e.dma_start(out=out, in_=in_), reads, writes, pwrites, dma=True)

    def fence(self, eng, reads):
        self.op(eng, None, reads, ())

    def barrier(self, out, in_):
        self.ops.append(Op("sp", lambda e: e.dma_start(out=out, in_=in_), (), (), (), True, barrier=True))

    def analyze(self):
        ops = self.ops
        cnt = {e: 0 for e in ENGS}
        dcnt = {e: 0 for e in ENGS}
        duse = {}
        last_barrier = None
        last_compute = {e: None for e in ENGS}
        dmas_since = []
        for i, o in enumerate(ops):
            o.eidx = cnt[o.eng]
            cnt[o.eng] += 1
            hard = set()
            war = set()
            if o.barrier:
                for e in ENGS:
                    if last_compute[e] is not None:
                        hard.add(last_compute[e])
                hard.update(dmas_since)
                dmas_since = []
                if last_barrier is not None:
                    hard.add(last_barrier)
                last_barrier = i
            else:
                if last_barrier is not None:
                    hard.add(last_barrier)
                for b in o.reads:
                    hard.update(b.ws)
                for b in o.writes:
                    hard.update(b.ws)
                    war.update(b.rd)
                    war.update(b.prd)
                for b in o.pwrites:
                    if b.rd:
                        b.prd = b.rd
                        b.rd = []
                        b.ws = []
                    war.update(b.prd)
                for b in o.reads:
                    b.rd.append(i)
                for b in o.writes:
                    b.ws = [i]
                    b.rd = []
                    b.prd = []
                for b in o.pwrites:
                    b.ws.append(i)
            war -= hard
            hard.discard(i)
            war.discard(i)
            keep = []
            for d in hard:
                od = ops[d]
                if (not od.dma) and (not o.dma) and od.eng == o.eng:
                    if o.eng == "pe":
                        continue
                    if o.eidx - od.eidx > 3:
                        continue
                keep.append(d)
            for d in war:
                od = ops[d]
                if (not od.dma) and od.eng == o.eng:
                    continue
                keep.append(d)
            best = {}
            keep2 = []
            for d in keep:
                od = ops[d]
                if od.dma:
                    keep2.append(d)
                elif best.get(od.eng, -1) < d:
                    best[od.eng] = d
            keep2.extend(best.values())
            o.deps = tuple(sorted(keep2))
            for d in o.deps:
                if not ops[d].dma:
                    ops[d].sig = True
            if o.dma:
                j = dcnt[o.eng] % self.n_dma_sems
                dcnt[o.eng] += 1
                key = (o.eng, j)
                o.dsem = key
                o.dprev = duse.get(key, 0)
                o.dval = o.dprev + 16
                duse[key] = o.dval
                dmas_since.append(i)
            elif o.fn is not None:
                last_compute[o.eng] = i
        sc = {e: 0 for e in ENGS}
        for o in ops:
            if o.sig:
                sc[o.eng] += 1
                o.sigval = sc[o.eng]

    def emit(self, stack):
        nc = self.nc
        self.analyze()
        ops = self.ops
        esem = {e: stack.enter_context(nc.semaphore("s_" + e)) for e in ENGS}
        dsem = {}
        for o in ops:
            if o.dma and o.dsem not in dsem:
                dsem[o.dsem] = stack.enter_context(nc.semaphore("d_%s%d" % o.dsem))
        block = stack.enter_context(nc.Block())
        per = {e: [o for o in ops if o.eng == e] for e in ENGS}

        def run(eng_name, engine):
            seen = {}
            for o in per[eng_name]:
                need = {}
                for d in o.deps:
                    od = ops[d]
                    if od.dma:
                        k, s, v = od.dsem, dsem[od.dsem], od.dval
                    else:
                        k, s, v = od.eng, esem[od.eng], od.sigval
                    if need.get(k, (None, 0))[1] < v:
                        need[k] = (s, v)
                if o.dma and o.dprev > 0:
                    k = o.dsem
                    if need.get(k, (None, 0))[1] < o.dprev:
                        need[k] = (dsem[k], o.dprev)
                for k, (s, v) in need.items():
                    if seen.get(k, 0) >= v:
                        continue
                    seen[k] = v
                    engine.wait_ge(s, v)
                if o.fn is None:
                    continue
                ins = o.fn(engine)
                if o.dma:
                    ins.then_inc(dsem[o.dsem], 16)
                elif o.sig:
                    ins.then_inc(esem[eng_name], 1)

        @block.tensor
        def _(e):
            run("pe", e)

        @block.scalar
        def _(e):
            run("act", e)

        @block.vector
        def _(e):
            run("dve", e)

        @block.gpsimd
        def _(e):
            run("pool", e)

        @block.sync
        def _(e):
            run("sp", e)


class Ring:
    def __init__(self, items):
        self.items = items
        self.i = 0

    def next(self):
        it = self.items[self.i % len(self.items)]
        self.i += 1
        return it


OC, OGM, OGF, OGN, OBA, OBF, OEPS, OKD, OMT, OQD = 0, 8, 40, 72, 80, 272, 288, 289, 297, 809
NFCOL = OQD + 1024
OMOD = NFCOL
NPF = OMOD + 5 * 64
OID, OON, OCT = 0, 128, 256
OW1 = OCT + 1024
NPB = OW1 + 64

_CONST = {}


def _consts():
    if _CONST:
        return _CONST
    bf = ml_dtypes.bfloat16
    p = np.arange(128)
    r_ = np.arange(32)
    q_ = np.arange(128)
    m_ = (p[:, None, None].astype(np.int64) * (32 * q_[None, None, :] + r_[None, :, None])) % 4096
    ang = m_.astype(np.float64) * (2.0 * np.pi / 4096.0)
    tw = np.stack([np.cos(ang) / 8.0, -np.sin(ang) / 8.0], axis=1)
    _CONST["twtab"] = np.ascontiguousarray(tw.reshape(128, 8192).astype(bf))
    a_ = np.arange(32)
    th = ((a_[:, None] * r_[None, :]) % 32).astype(np.float64) * (2.0 * np.pi / 32.0)
    w1 = np.zeros((64, 64), dtype=np.float64)
    w1[0:32, 0:32] = np.cos(th)
    w1[32:64, 0:32] = -np.sin(th)
    w1[0:32, 32:64] = np.sin(th)
    w1[32:64, 32:64] = np.cos(th)
    w1 /= 8.0
    cin = (np.arange(2)[None, :, None] * 128 + p[:, None, None]).astype(np.int64)
    cout = np.arange(256)[None, None, :]
    ang = ((cin * cout) % 256).astype(np.float64) * (2.0 * np.pi / 256.0)
    ct = np.stack([np.cos(ang) / 16.0, np.sin(ang) / 16.0], axis=1)
    sb = np.zeros((128, NPB), dtype=bf)
    sb[:, OID:OID + 128] = np.eye(128).astype(bf)
    sb[:, OON:OON + 128] = np.full((128, 128), 1.0 / 1024.0).astype(bf)
    sb[:, OCT:OCT + 1024] = ct.reshape(128, 1024).astype(bf)
    sb[0:64, OW1:OW1 + 64] = w1.astype(bf)
    _CONST["smallb"] = sb
    inv = 10000.0 ** (-np.arange(128, dtype=np.float64) / 128.0)
    ang = inv[:, None] * np.arange(S, dtype=np.float64)[None, :]
    _CONST["rot"] = np.stack([np.cos(ang), np.sin(ang)], axis=1).astype(np.float32)
    hidx = np.arange(H, dtype=np.float64)
    lgf = np.log1p(-np.exp2(-5.0 - hidx))
    lgb = lgf[::-1].copy()
    sf = np.zeros((128, NFCOL), dtype=np.float32)
    sf[:, OEPS] = EPS
    s_ = np.arange(CH, dtype=np.float64)
    for h in range(H):
        sf[:, OKD + h] = np.exp((CH - 1.0 - s_) * lgf[h])
        sf[:, OKD + 4 + h] = np.exp(s_ * lgb[h])
        diff = s_[None, :] - s_[:, None]
        mt = np.where(diff >= 0, np.exp(np.where(diff >= 0, diff, 0) * lgf[h]),
                      np.exp(np.where(diff < 0, -diff, 0) * lgb[h]))
        sf[:, OMT + h * 128:OMT + (h + 1) * 128] = mt
        sf[:, OQD + h * 128:OQD + (h + 1) * 128] = np.exp((s_ + 1.0) * lgf[h])[None, :]
        sf[:, OQD + 512 + h * 128:OQD + 512 + (h + 1) * 128] = np.exp((CH - s_) * lgb[h])[None, :]
    _CONST["sf"] = sf
    _CONST["cdf"] = [float(np.exp(CH * lgf[h])) for h in range(H)]
    _CONST["cdb"] = [float(np.exp(CH * lgb[h])) for h in range(H)]
    return _CONST


ARENA_WORDS = 49152


class Builder:
    def __init__(self, nsub=8, final=True):
        self.nsub = nsub
        self.final = final
        self.nc = nc = bass.Bass("TRN2", target_bir_lowering=False)
        din = lambda n, s, dt: nc.dram_tensor(n, s, dt, kind="ExternalInput").ap()
        dsc = lambda n, s, dt: nc.dram_tensor(n, s, dt, kind="Internal").ap()
        self.xin = din("xT", [D, S], F32)
        self.smallf = din("smallf", [128, NFCOL], F32)
        self.smallb = din("smallb", [128, NPB], BF16)
        self.twtab = din("twtab", [128, 8192], BF16)
        self.rot = din("rot", [128, 2, S], F32)
        self.w_ada = din("w_ada", [NL, D, 6 * D], F32)
        self.w_ada_final = din("w_ada_final", [D, 2 * D], F32)
        self.w_four = din("w_fourier_out", [2, D, D], F32)
        self.w_ret_in = din("w_ret_in", [2, D, 6 * D], F32)
        self.w_ret_out = din("w_ret_out", [2, 2 * D, D], F32)
        self.w_ffn_in = din("w_ffn_in", [NL, D, 2 * FF], F32)
        self.w_ffn_out = din("w_ffn_out", [NL, FF, D], F32)
        self.out = nc.dram_tensor("outT", [D, S], F32, kind="ExternalOutput").ap()
        self.xres = dsc("xres", [D, S], F32)
        self.Gd = [dsc("Gc", [S, D], BF16), dsc("Gs", [S, D], BF16)]
        self.Bd = dsc("Bd", [2, 32, 128, D], BF16)
        self.hTd = dsc("hTd", [D, S], BF16)
        self.qTd = dsc("qTd", [D, S], BF16)
        self.kTd = dsc("kTd", [D, S], BF16)
        self.ktokd = dsc("ktokd", [S, D], BF16)
        self.vd = dsc("vd", [S, 2 * D], BF16)
        self.sgd = dsc("sgd", [S, 2 * D], BF16)
        self.RBd = dsc("RBd", [H, NCH, DK, DV], BF16)
        self.zTd = dsc("zTd", [2 * D, S], BF16)
        self.dmy = dsc("dmy", [2, 64], F32)
        import os
        self.dbgz = None
        if os.environ.get("KDBGZ"):
            self.dbgz = nc.dram_tensor("dbgz", [2 * D, S], BF16, kind="ExternalOutput").ap()
            self.dbgq = nc.dram_tensor("dbgq", [D, S], BF16, kind="ExternalOutput").ap()
            self.dbgr = nc.dram_tensor("dbgr", [H, NCH, DK, DV], BF16, kind="ExternalOutput").ap()
        self.bX = [Buf("x%d" % i) for i in range(8)]
        self.bOut = Buf("out")
        self.bscr = {n: Buf(n) for n in ("G", "B", "hT", "qT", "kT", "ktok", "v", "sg", "RB", "zT")}

    def reset(self):
        self.aoff = 0

    def alloc(self, free_shape, dt):
        n = 1
        for s_ in free_shape:
            n *= s_
        nw = (n * (2 if dt == BF16 else 4) + 3) // 4
        assert self.aoff + nw <= ARENA_WORDS, ("arena overflow", self.aoff, nw)
        a = self.arena[:, self.aoff:self.aoff + nw]
        self.aoff += nw
        if dt != F32:
            a = a.bitcast(dt)
        if len(free_shape) == 2:
            a = a.rearrange("p (a b) -> p a b", a=free_shape[0])
        elif len(free_shape) == 3:
            a = a.rearrange("p (a b c) -> p a b c", a=free_shape[0], b=free_shape[1])
        return a

    def ring(self, n, free_shape, dt, name="r"):
        return Ring([(self.alloc(free_shape, dt), Buf(name + str(i))) for i in range(n)])

    def next_ps(self):
        i = self.psi % 8
        self.psi += 1
        return self.ps[i], self.psb[i]

    def phase_barrier(self):
        self.P.barrier(self.dmy[1:2, :], self.smallf[0:1, 0:64])
        self.reset()

    def modcol(self, li, which):
        base = OMOD + li * 64
        off = {"B1": 0, "G1": 16, "B2": 24, "G2": 40, "A1": 48, "A2": 56, "BF": 0, "AF": 48}[which]
        return self.pf[:, base + off:base + off + 8]

    def build(self):
        nc = self.nc
        with contextlib.ExitStack() as st:
            self.arena = st.enter_context(nc.sbuf_tensor("arena", [128, ARENA_WORDS], F32))
            self.pf = st.enter_context(nc.sbuf_tensor("pf", [128, NPF], F32))
            self.pb = st.enter_context(nc.sbuf_tensor("pb", [128, NPB], BF16))
            self.ps = [st.enter_context(nc.psum_tensor("ps%d" % i, [128, 512], F32)) for i in range(8)]
            self.psb = [Buf("ps%d" % i) for i in range(8)]
            self.psi = 0
            self.bpf = Buf("pf")
            self.bpb = Buf("pb")
            self.bmod = Buf("mod")
            self.P = P = Prog(nc)
            self.reset()
            P.dma("sp", self.pf[:, 0:NFCOL], self.smallf[:, :], writes=[self.bpf])
            P.dma("sp", self.pb[:, :], self.smallb[:, :], writes=[self.bpb])
            self.ident = self.pb[:, OID:OID + 128]
            self.onesD = self.pb[:, OON:OON + 128]
            self.eps = self.pf[:, OEPS:OEPS + 1]
            self.adaln()
            subs = []
            for li in range(NL):
                subs.append(("mix", li))
                subs.append(("ffn", li))
            for kind, li in subs[:self.nsub]:
                self.phase_barrier()
                xs = self.xin if (kind == "mix" and li == 0) else self.xres
                if kind == "ffn":
                    self.ffn(li)
                elif li % 2 == 0:
                    self.fourier(li, xs)
                else:
                    self.retention(li, xs)
            self.phase_barrier()
            if self.final:
                self.final_norm()
            else:
                src = self.xres if self.nsub > 0 else self.xin
                P.dma("sp", self.out[:, :], src[:, :], reads=self.bX, pwrites=[self.bOut])
            if self.dbgz is not None:
                P.dma("sp", self.dbgz[:, :], self.zTd[:, :], pwrites=[self.bOut])
                P.dma("sp", self.dbgq[:, :], self.qTd[:, :], pwrites=[self.bOut])
                for h_ in range(H):
                    P.dma("sp", self.dbgr[h_].rearrange("n d v -> n (d v)"), self.RBd[h_].rearrange("n d v -> n (d v)"), pwrites=[self.bOut])
            P.fence("sp", [self.bOut])
            P.emit(st)
        return nc

    def adaln(self):
        P = self.P
        pf = self.pf
        cact = self.alloc([8], F32)
        cactb = self.alloc([8], BF16)
        bca, bcb = Buf("cact"), Buf("cactb")
        P.op("act", lambda e: e.activation(out=cact, in_=pf[:, OC:OC + 8], func=AF.Silu), [self.bpf], [bca])
        P.op("dve", lambda e: e.tensor_copy(out=cactb, in_=cact), [bca], [bcb])
        wring = self.ring(2, [8, 1536], BF16, "wada")
        for li in range(NL + 1):
            ncols = 6 * D if li < NL else 2 * D
            qw = 1536 if li < NL else 1024
            src = (self.w_ada[li] if li < NL else self.w_ada_final).rearrange("(kc p) n -> p kc n", p=128)
            ps, bps = self.next_ps()
            for q in range(ncols // qw):
                wt, bw = wring.next()
                P.dma("pool", wt[:, :, 0:qw], src[:, :, q * qw:(q + 1) * qw], writes=[bw])
                for jc in range(qw // 128):
                    col = q * (qw // 128) + jc
                    for kc in range(8):
                        P.op("pe", lambda e, ps=ps, wt=wt, col=col, jc=jc, kc=kc: e.matmul(
                            ps[:, col:col + 1], lhsT=wt[:, kc, jc * 128:(jc + 1) * 128], rhs=cactb[:, kc:kc + 1],
                            start=(kc == 0), stop=(kc == 7)), [bw, bcb], pwrites=[bps])
            nm = ncols // 128
            base = OMOD + li * 64
            boff = OBA + li * 48 if li < NL else OBF
            P.op("dve", lambda e, ps=ps, nm=nm, base=base, boff=boff: e.tensor_tensor(
                out=pf[:, base:base + nm], in0=ps[:, 0:nm], in1=pf[:, boff:boff + nm], op=ALU.add),
                [bps, self.bpf], pwrites=[self.bmod])
            if li < NL:
                P.op("dve", lambda e, base=base, li=li: e.scalar_tensor_tensor(
                    out=pf[:, base + 48:base + 56], in0=pf[:, base + 8:base + 16], scalar=1.0,
                    in1=pf[:, OGM + li * 8:OGM + li * 8 + 8], op0=ALU.add, op1=ALU.mult),
                    [self.bmod, self.bpf], pwrites=[self.bmod])
                P.op("dve", lambda e, base=base, li=li: e.scalar_tensor_tensor(
                    out=pf[:, base + 56:base + 64], in0=pf[:, base + 32:base + 40], scalar=1.0,
                    in1=pf[:, OGF + li * 8:OGF + li * 8 + 8], op0=ALU.add, op1=ALU.mult),
                    [self.bmod, self.bpf], pwrites=[self.bmod])
            else:
                P.op("dve", lambda e, base=base: e.scalar_tensor_tensor(
                    out=pf[:, base + 48:base + 56], in0=pf[:, base + 8:base + 16], scalar=1.0,
                    in1=pf[:, OGN:OGN + 8], op0=ALU.add, op1=ALU.mult),
                    [self.bmod, self.bpf], pwrites=[self.bmod])

    def norm_setup(self):
        self.n_sq = self.ring(1, [8, 512], BF16, "sq")
        self.n_rstd = self.ring(2, [512], F32, "rstd")
        self.n_tmp = self.ring(3, [512], F32, "ntmp")

    def norm_block(self, x3, bx, blk, A, B, out3, bout):
        P = self.P
        sl = slice(blk * 512, (blk + 1) * 512)
        sq, bsq = self.n_sq.next()
        rstd, brs = self.n_rstd.next()
        ps, bps = self.next_ps()
        P.op("act", lambda e: e.activation(out=sq, in_=x3[:, :, sl], func=AF.Square), [bx], [bsq])
        for kc in range(8):
            P.op("pe", lambda e, kc=kc: e.matmul(ps[:, :], lhsT=self.onesD, rhs=sq[:, kc, :],
                                                   start=(kc == 0), stop=(kc == 7)), [bsq, self.bpb], pwrites=[bps])
        P.op("act", lambda e: e.activation(out=rstd, in_=ps[:, :], func=AF.Sqrt, bias=self.eps, scale=1.0),
             [bps, self.bpf], [brs])
        P.op("dve", lambda e: e.reciprocal(out=rstd, in_=rstd), [brs], [brs])
        for kc in range(8):
            tmp, btmp = self.n_tmp.next()
            P.op("dve", lambda e, kc=kc, tmp=tmp: e.scalar_tensor_tensor(
                out=tmp, in0=x3[:, kc, sl], scalar=A[:, kc:kc + 1], in1=rstd, op0=ALU.mult, op1=ALU.mult),
                [bx, brs, self.bmod], [btmp])
            P.op("act", lambda e, kc=kc, tmp=tmp: e.activation(
                out=out3[:, kc, sl], in_=tmp, func=AF.Identity, bias=B[:, kc:kc + 1], scale=1.0),
                [btmp, self.bmod], pwrites=[bout])

    def ffn(self, li):
        P = self.P
        A2, B2, G2 = self.modcol(li, "A2"), self.modcol(li, "B2"), self.modcol(li, "G2")
        wi_v = self.w_ffn_in[li].rearrange("(kc p) n -> p kc n", p=128)
        wo_v = self.w_ffn_out[li].rearrange("(j p) n -> p j n", p=128)
        xv = self.xres.rearrange("(kc p) t -> p kc t", p=128)
        self.norm_setup()
        T = 1024
        NT = S // T
        xring = self.ring(1, [8, T], F32, "x")
        hring = self.ring(2, [8, T], BF16, "hT")
        act, bact = self.alloc([22, T], BF16), Buf("act")
        wgr = self.ring(3, [2, 8, 256], BF16, "wgu")
        wor = self.ring(2, [22, 256], BF16, "wo")
        sil = self.ring(2, [512], F32, "sil")
        xor_ = self.ring(3, [2, 512], F32, "xo")

        def load_norm(tt):
            t0 = tt * T
            x3, bx = xring.next()
            P.dma("sp", x3, xv[:, :, t0:t0 + T], reads=[self.bX[2 * tt], self.bX[2 * tt + 1]], writes=[bx])
            hT, bh = hring.next()
            for blk in range(T // 512):
                self.norm_block(x3, bx, blk, A2, B2, hT, bh)
            return hT, bh

        cur = load_norm(0)
        for tt in range(NT):
            t0 = tt * T
            hT, bh = cur
            xb = [self.bX[2 * tt], self.bX[2 * tt + 1]]
            for g in range(11):
                w, bw = wgr.next()
                P.dma("pool", w[:, 0], wi_v[:, :, g * 256:(g + 1) * 256], pwrites=[bw])
                P.dma("pool", w[:, 1], wi_v[:, :, FF + g * 256:FF + (g + 1) * 256], pwrites=[bw])
                for jc in range(2):
                    j = g * 2 + jc
                    for blk in range(T // 512):
                        sl = slice(blk * 512, (blk + 1) * 512)
                        psg, bpg = self.next_ps()
                        psu, bpu = self.next_ps()
                        for gu, ps, bps in ((0, psg, bpg), (1, psu, bpu)):
                            for kc in range(8):
                                P.op("pe", lambda e, w=w, gu=gu, ps=ps, kc=kc, jc=jc, sl=sl, hT=hT: e.matmul(
                                    ps[:, :], lhsT=w[:, gu, kc, jc * 128:(jc + 1) * 128], rhs=hT[:, kc, sl],
                                    start=(kc == 0), stop=(kc == 7)), [bw, bh], pwrites=[bps])
                        sg, bsg = sil.next()
                        P.op("act", lambda e, sg=sg, psg=psg: e.activation(out=sg, in_=psg[:, :], func=AF.Silu),
                             [bpg], [bsg])
                        P.op("dve", lambda e, sg=sg, psu=psu, j=j, sl=sl: e.tensor_tensor(
                            out=act[:, j, sl], in0=sg, in1=psu[:, :], op=ALU.mult), [bsg, bpu], pwrites=[bact])
            if tt + 1 < NT:
                cur = load_norm(tt + 1)
            for og in range(4):
                w, bw = wor.next()
                P.dma("pool", w, wo_v[:, :, og * 256:(og + 1) * 256], writes=[bw])
                for blk in range(T // 512):
                    sl = slice(blk * 512, (blk + 1) * 512)
                    xo, bxo = xor_.next()
                    P.dma("sp", xo, xv[:, og * 2:og * 2 + 2, t0 + blk * 512:t0 + (blk + 1) * 512], reads=xb, writes=[bxo])
                    for oc2 in range(2):
                        oc = og * 2 + oc2
                        ps, bps = self.next_ps()
                        for j in range(22):
                            P.op("pe", lambda e, w=w, ps=ps, j=j, oc2=oc2, sl=sl: e.matmul(
                                ps[:, :], lhsT=w[:, j, oc2 * 128:(oc2 + 1) * 128], rhs=act[:, j, sl],
                                start=(j == 0), stop=(j == 21)), [bw, bact], pwrites=[bps])
                        P.op("dve", lambda e, ps=ps, oc=oc, oc2=oc2, xo=xo: e.scalar_tensor_tensor(
                            out=xo[:, oc2, :], in0=ps[:, :], scalar=G2[:, oc:oc + 1], in1=xo[:, oc2, :],
                            op0=ALU.mult, op1=ALU.add), [bps, self.bmod, bxo], pwrites=[bxo])
                    P.dma("sp", xv[:, og * 2:og * 2 + 2, t0 + blk * 512:t0 + (blk + 1) * 512], xo, reads=[bxo], pwrites=xb)

    def fourier(self, li, xsrc):
        P = self.P
        j = li // 2
        A1, B1, G1 = self.modcol(li, "A1"), self.modcol(li, "B1"), self.modcol(li, "G1")
        xv_in = xsrc.rearrange("(kc p) t -> p kc t", p=128)
        xv = self.xres.rearrange("(kc p) t -> p kc t", p=128)
        chan = self.pb[:, OCT:OCT + 1024].rearrange("p (cs cc n) -> p cs cc n", cs=2, cc=2)
        Gv = [g.rearrange("(a p) f -> p a f", p=128) for g in self.Gd]
        bG = self.bscr["G"]
        self.norm_setup()
        wo, bwo = self.alloc([8, 1024], BF16), [Buf("wo0"), Buf("wo1")]
        wsrc = self.w_four[j].rearrange("(kc p) n -> p kc n", p=128)
        for half in range(2):
            P.dma("pool", wo[:, :, half * 512:(half + 1) * 512], wsrc[:, :, half * 512:(half + 1) * 512],
                  writes=[bwo[half]])
        xring = self.ring(2, [8, 512], F32, "x")
        hring = self.ring(2, [8, 512], BF16, "hT")
        Yr = [self.ring(2, [8, 512], BF16, "Yc"), self.ring(2, [8, 512], BF16, "Ys")]
        gst = [self.ring(2, [4, 1024], BF16, "gstc"), self.ring(2, [4, 1024], BF16, "gsts")]
        flip = 0
        def f1_load(tt):
            x3, bx = xring.next()
            P.dma("sp", x3, xv_in[:, :, tt * 512:(tt + 1) * 512], reads=[self.bX[tt]], writes=[bx])
            return x3, bx

        nxt = f1_load(0)
        for tt in range(8):
            t0 = tt * 512
            x3, bx = nxt
            if tt + 1 < 8:
                nxt = f1_load(tt + 1)
            hT, bh = hring.next()
            Y = [Yr[0].next(), Yr[1].next()]
            self.norm_block(x3, bx, 0, A1, B1, hT, bh)
            for cs in range(2):
                Yt, bY = Y[cs]
                for oc in range(8):
                    g, ocl = oc // 2, oc % 2
                    ps, bps = self.next_ps()
                    for cc in range(2):
                        P.op("pe", lambda e, ps=ps, cs=cs, cc=cc, ocl=ocl, g=g, hT=hT: e.matmul(
                            ps[:, :], lhsT=chan[:, cs, cc, ocl * 128:(ocl + 1) * 128], rhs=hT[:, 2 * g + cc, :],
                            start=(cc == 0), stop=(cc == 1)), [self.bpb, bh], pwrites=[bps])
                    flip ^= 1
                    if flip:
                        P.op("act", lambda e, ps=ps, Yt=Yt, oc=oc: e.activation(out=Yt[:, oc, :], in_=ps[:, :], func=AF.Identity),
                             [bps], pwrites=[bY])
                    else:
                        P.op("dve", lambda e, ps=ps, Yt=Yt, oc=oc: e.tensor_copy(out=Yt[:, oc, :], in_=ps[:, :]),
                             [bps], pwrites=[bY])
            for cs in range(2):
                Yt, bY = Y[cs]
                go, bgo = gst[cs].next()
                for s4 in range(4):
                    for half in range(2):
                        ps, bps = self.next_ps()
                        for oc in range(8):
                            P.op("pe", lambda e, ps=ps, Yt=Yt, oc=oc, s4=s4, half=half: e.matmul(
                                ps[:, :], lhsT=Yt[:, oc, s4 * 128:(s4 + 1) * 128], rhs=wo[:, oc, half * 512:(half + 1) * 512],
                                start=(oc == 0), stop=(oc == 7)), [bY, bwo[half]], pwrites=[bps])
                        flip ^= 1
                        if flip:
                            P.op("act", lambda e, ps=ps, go=go, s4=s4, half=half: e.activation(
                                out=go[:, s4, half * 512:(half + 1) * 512], in_=ps[:, :], func=AF.Identity), [bps], pwrites=[bgo])
                        else:
                            P.op("dve", lambda e, ps=ps, go=go, s4=s4, half=half: e.tensor_copy(
                                out=go[:, s4, half * 512:(half + 1) * 512], in_=ps[:, :]), [bps], pwrites=[bgo])
                P.dma("sp", Gv[cs][:, tt * 4:(tt + 1) * 4, :], go, reads=[bgo], pwrites=[bG])
        self.phase_barrier()
        bB = self.bscr["B"]
        PB = 8
        W1 = self.pb[0:64, OW1:OW1 + 64]
        Gap = [g.rearrange("(a p) f -> a p f", p=128) for g in self.Gd]
        Bst_v = self.Bd.rearrange("c r p f -> (c r) p f")
        Bld_v = self.Bd.rearrange("c r p f -> p c r f")
        tw, btw = self.alloc([2, 32, 128], BF16), Buf("tw")
        P.dma("sp", tw.rearrange("p a b c -> p (a b c)"), self.twtab[:, :], writes=[btw])
        yir = self.ring(2, [PB, 1024], BF16, "yin")
        bor = self.ring(2, [PB, 1024], BF16, "bst")
        binr = self.ring(2, [2, 32, 128], BF16, "bin")
        xfr = self.ring(2, [S], F32, "xf")
        flip = 0
        def f2a_load(pb_):
            yi, byi = yir.next()
            P.dma("sp", yi[0:32], Gap[0][:, pb_ * PB:(pb_ + 1) * PB, :], reads=[bG], pwrites=[byi])
            P.dma("sp", yi[32:64], Gap[1][:, pb_ * PB:(pb_ + 1) * PB, :], reads=[bG], pwrites=[byi])
            return yi, byi

        nxt = f2a_load(0)
        for pb_ in range(128 // PB):
            yi, byi = nxt
            if pb_ + 1 < 128 // PB:
                nxt = f2a_load(pb_ + 1)
            bo, bbo = bor.next()
            for pi in range(PB):
                for half in range(2):
                    ps, bps = self.next_ps()
                    P.op("pe", lambda e, ps=ps, yi=yi, pi=pi, half=half: e.matmul(
                        ps[0:64, :], lhsT=W1, rhs=yi[0:64, pi, half * 512:(half + 1) * 512], start=True, stop=True),
                        [self.bpb, byi], [bps])
                    flip ^= 1
                    if flip:
                        P.op("act", lambda e, ps=ps, bo=bo, pi=pi, half=half: e.activation(
                            out=bo[0:64, pi, half * 512:(half + 1) * 512], in_=ps[0:64, :], func=AF.Identity),
                            [bps], pwrites=[bbo])
                    else:
                        P.op("dve", lambda e, ps=ps, bo=bo, pi=pi, half=half: e.tensor_copy(
                            out=bo[0:64, pi, half * 512:(half + 1) * 512], in_=ps[0:64, :]), [bps], pwrites=[bbo])
            P.dma("sp", Bst_v[:, pb_ * PB:(pb_ + 1) * PB, :], bo[0:64], reads=[bbo], pwrites=[bB])
        def f2b_load(fc):
            bi, bbi = binr.next()
            P.dma("sp", bi, Bld_v[:, :, :, fc * 128:(fc + 1) * 128], reads=[bB], writes=[bbi])
            xf, bxf = xfr.next()
            P.dma("sp", xf, xv_in[:, fc, :], reads=self.bX, writes=[bxf])
            return bi, bbi, xf, bxf

        nxt = f2b_load(0)
        for fc in range(8):
            bi, bbi, xf, bxf = nxt
            if fc + 1 < 8:
                nxt = f2b_load(fc + 1)
            xf3 = xf.rearrange("p (q r) -> p r q", r=32)
            for rg in range(8):
                ps, bps = self.next_ps()
                for r4 in range(4):
                    r = rg * 4 + r4
                    for cs in range(2):
                        P.op("pe", lambda e, ps=ps, bi=bi, r=r, r4=r4, cs=cs: e.matmul(
                            ps[:, r4 * 128:(r4 + 1) * 128], lhsT=bi[:, cs, r, :], rhs=tw[:, cs, r, :],
                            start=(cs == 0), stop=(cs == 1)), [bbi, btw], pwrites=[bps])
                P.op("dve", lambda e, ps=ps, xf3=xf3, rg=rg, fc=fc: e.scalar_tensor_tensor(
                    out=xf3[:, rg * 4:(rg + 1) * 4, :], in0=ps[:, :].rearrange("p (a b) -> p a b", a=4),
                    scalar=G1[:, fc:fc + 1], in1=xf3[:, rg * 4:(rg + 1) * 4, :], op0=ALU.mult, op1=ALU.add),
                    [bps, self.bmod, bxf], pwrites=[bxf])
            P.dma("sp", xv[:, fc, :], xf, reads=[bxf], pwrites=self.bX)

    def retention(self, li, xsrc):
        P = self.P
        pf = self.pf
        j = li // 2
        C = _consts()
        A1, B1, G1 = self.modcol(li, "A1"), self.modcol(li, "B1"), self.modcol(li, "G1")
        xv = self.xres.rearrange("(kc p) t -> p kc t", p=128)
        win = self.w_ret_in[j].rearrange("(kc p) n -> p kc n", p=128)
        hTv = self.hTd.rearrange("(kc p) t -> p kc t", p=128)
        qTv = self.qTd.rearrange("(kc p) t -> p kc t", p=128)
        kTv = self.kTd.rearrange("(kc p) t -> p kc t", p=128)
        ktv = self.ktokd.rearrange("(a p) f -> p a f", p=128)
        vv = self.vd.rearrange("(a p) f -> p a f", p=128)
        sgv = self.sgd.rearrange("(a p) f -> p a f", p=128)
        zTv = self.zTd.rearrange("(c p) t -> p c t", p=128)
        bs = self.bscr
        self.norm_setup()
        wq, bwq = self.alloc([8, 1024], BF16), [Buf("wq%d" % i) for i in range(H)]
        wk, bwk = self.alloc([8, 1024], BF16), [Buf("wk%d" % i) for i in range(H)]
        for h_ in range(H):
            P.dma("pool", wq[:, :, h_ * 256:(h_ + 1) * 256], win[:, :, h_ * 256:(h_ + 1) * 256], writes=[bwq[h_]])
        for h_ in range(H):
            P.dma("pool", wk[:, :, h_ * 256:(h_ + 1) * 256], win[:, :, 1024 + h_ * 256:1024 + (h_ + 1) * 256],
                  writes=[bwk[h_]])
        xring = self.ring(2, [8, 512], F32, "x")
        hring = self.ring(2, [8, 512], BF16, "hT")
        rotr = self.ring(2, [2, 512], F32, "rot")
        x12 = self.ring(3, [2, 512], F32, "x12")
        tq = self.ring(3, [4, 512], F32, "tq")
        qor = self.ring(2, [8, 512], BF16, "qo")
        kor = self.ring(2, [8, 512], BF16, "ko")
        ktr = self.ring(2, [4, 1024], BF16, "kt")
        def r1a_load(tt):
            x3, bx = xring.next()
            P.dma("sp", x3, xv[:, :, tt * 512:(tt + 1) * 512], reads=[self.bX[tt]], writes=[bx])
            rt, brt = rotr.next()
            P.dma("sp", rt, self.rot[:, :, tt * 512:(tt + 1) * 512], writes=[brt])
            return x3, bx, rt, brt

        nxt = r1a_load(0)
        for tt in range(8):
            t0 = tt * 512
            x3, bx, rt, brt = nxt
            if tt + 1 < 8:
                nxt = r1a_load(tt + 1)
            hT, bh = hring.next()
            self.norm_block(x3, bx, 0, A1, B1, hT, bh)
            P.dma("sp", hTv[:, :, t0:t0 + 512], hT, reads=[bh], pwrites=[bs["hT"]])
            qo, bqo = qor.next()
            ko, bko = kor.next()
            for (w, bw, oT, boT, scl) in ((wq, bwq, qo, bqo, 1.0), (wk, bwk, ko, bko, 1.0 / 16.0)):
                for h in range(H):
                    pss = []
                    for half in range(2):
                        oc = 2 * h + half
                        ps, bps = self.next_ps()
                        pss.append((ps, bps))
                        for kc in range(8):
                            P.op("pe", lambda e, ps=ps, w=w, kc=kc, oc=oc, hT=hT: e.matmul(
                                ps[:, :], lhsT=w[:, kc, oc * 128:(oc + 1) * 128], rhs=hT[:, kc, :],
                                start=(kc == 0), stop=(kc == 7)), [bw[h], bh], pwrites=[bps])
                    xx, bxx = x12.next()
                    for half in range(2):
                        P.op("act", lambda e, xx=xx, half=half, ps=pss[half][0], scl=scl: e.activation(
                            out=xx[:, half, :], in_=ps[:, :], func=AF.Identity, scale=scl), [pss[half][1]], pwrites=[bxx])
                    t4, bt4 = tq.next()
                    for ti, (xi, ci) in enumerate(((0, 0), (1, 1), (0, 1), (1, 0))):
                        P.op("dve", lambda e, t4=t4, ti=ti, xx=xx, xi=xi, ci=ci, rt=rt: e.tensor_tensor(
                            out=t4[:, ti, :], in0=xx[:, xi, :], in1=rt[:, ci, :], op=ALU.mult), [bxx, brt], pwrites=[bt4])
                    P.op("pool", lambda e, t4=t4, oT=oT, h=h: e.tensor_tensor(
                        out=oT[:, 2 * h, :], in0=t4[:, 0, :], in1=t4[:, 1, :], op=ALU.subtract), [bt4], pwrites=[boT])
                    P.op("pool", lambda e, t4=t4, oT=oT, h=h: e.tensor_tensor(
                        out=oT[:, 2 * h + 1, :], in0=t4[:, 2, :], in1=t4[:, 3, :], op=ALU.add), [bt4], pwrites=[boT])
            P.dma("sp", qTv[:, :, t0:t0 + 512], qo, reads=[bqo], pwrites=[bs["qT"]])
            P.dma("sp", kTv[:, :, t0:t0 + 512], ko, reads=[bko], pwrites=[bs["kT"]])
            kt, bkt = ktr.next()
            for s4 in range(4):
                ps, bps = self.next_ps()
                psT = ps[:, :].bitcast(BF16)
                for c in range(8):
                    P.op("pe", lambda e, psT=psT, ko=ko, c=c, s4=s4: e.transpose(
                        psT[:, c * 128:(c + 1) * 128], ko[:, c, s4 * 128:(s4 + 1) * 128], self.ident),
                        [bko, self.bpb], pwrites=[bps])
                P.op("act", lambda e, psT=psT, kt=kt, s4=s4: e.activation(out=kt[:, s4, :], in_=psT[:, 0:1024], func=AF.Identity),
                     [bps], pwrites=[bkt])
            P.dma("sp", ktv[:, tt * 4:(tt + 1) * 4, :], kt, reads=[bkt], pwrites=[bs["ktok"]])
        self.phase_barrier()
        wv, bwv = self.alloc([8, 2048], BF16), [Buf("wv%d" % i) for i in range(4)]
        wg, bwg = self.alloc([8, 2048], BF16), [Buf("wg%d" % i) for i in range(4)]
        for blk in range(4):
            P.dma("pool", wv[:, :, blk * 512:(blk + 1) * 512], win[:, :, 2048 + blk * 512:2048 + (blk + 1) * 512],
                  writes=[bwv[blk]])
            P.dma("pool", wg[:, :, blk * 512:(blk + 1) * 512], win[:, :, 4096 + blk * 512:4096 + (blk + 1) * 512],
                  writes=[bwg[blk]])
        hring = self.ring(2, [8, 512], BF16, "hT")
        vor = self.ring(2, [4, 2048], BF16, "vo")
        gor = self.ring(2, [4, 2048], BF16, "go")
        flip = 0
        def r1b_load(tt):
            hT, bh = hring.next()
            P.dma("sp", hT, hTv[:, :, tt * 512:(tt + 1) * 512], reads=[bs["hT"]], writes=[bh])
            return hT, bh

        nxt = r1b_load(0)
        for tt in range(8):
            t0 = tt * 512
            hT, bh = nxt
            if tt + 1 < 8:
                nxt = r1b_load(tt + 1)
            vo, bvo = vor.next()
            go, bgo = gor.next()
            for s4 in range(4):
                for isg, (w, bw, o_, bo_) in enumerate(((wv, bwv, vo, bvo), (wg, bwg, go, bgo))):
                    for blk in range(4):
                        ps, bps = self.next_ps()
                        for kc in range(8):
                            P.op("pe", lambda e, ps=ps, hT=hT, w=w, kc=kc, s4=s4, blk=blk: e.matmul(
                                ps[:, :], lhsT=hT[:, kc, s4 * 128:(s4 + 1) * 128], rhs=w[:, kc, blk * 512:(blk + 1) * 512],
                                start=(kc == 0), stop=(kc == 7)), [bh, bw[blk]], pwrites=[bps])
                        if isg:
                            P.op("act", lambda e, ps=ps, o_=o_, s4=s4, blk=blk: e.activation(
                                out=o_[:, s4, blk * 512:(blk + 1) * 512], in_=ps[:, :], func=AF.Silu), [bps], pwrites=[bo_])
                        else:
                            P.op("dve", lambda e, ps=ps, o_=o_, s4=s4, blk=blk: e.tensor_copy(
                                out=o_[:, s4, blk * 512:(blk + 1) * 512], in_=ps[:, :]), [bps], pwrites=[bo_])
            P.dma("sp", vv[:, tt * 4:(tt + 1) * 4, :], vo, reads=[bvo], pwrites=[bs["v"]])
            P.dma("sp", sgv[:, tt * 4:(tt + 1) * 4, :], go, reads=[bgo], pwrites=[bs["sg"]])
        self.phase_barrier()
        self.ret_core(qTv, kTv, ktv, vv, sgv, zTv)
        self.phase_barrier()
        wo, bwo = self.alloc([16, 1024], BF16), [Buf("wo%d" % i) for i in range(4)]
        wsrc = self.w_ret_out[j].rearrange("(k p) n -> p k n", p=128)
        for og in range(4):
            P.dma("pool", wo[:, :, og * 256:(og + 1) * 256], wsrc[:, :, og * 256:(og + 1) * 256], writes=[bwo[og]])
        xring = self.ring(2, [8, 512], F32, "x")
        zring = self.ring(2, [16, 512], BF16, "zT")
        def r3_load(tt):
            x3, bx = xring.next()
            P.dma("sp", x3, xv[:, :, tt * 512:(tt + 1) * 512], reads=[self.bX[tt]], writes=[bx])
            zt, bzt = zring.next()
            P.dma("sp", zt, zTv[:, :, tt * 512:(tt + 1) * 512], reads=[bs["zT"]], writes=[bzt])
            return x3, bx, zt, bzt

        nxt = r3_load(0)
        for tt in range(8):
            t0 = tt * 512
            x3, bx, zt, bzt = nxt
            if tt + 1 < 8:
                nxt = r3_load(tt + 1)
            for oc in range(8):
                ps, bps = self.next_ps()
                for k in range(16):
                    P.op("pe", lambda e, ps=ps, k=k, oc=oc, zt=zt: e.matmul(
                        ps[:, :], lhsT=wo[:, k, oc * 128:(oc + 1) * 128], rhs=zt[:, k, :],
                        start=(k == 0), stop=(k == 15)), [bwo[oc // 2], bzt], pwrites=[bps])
                P.op("dve", lambda e, ps=ps, oc=oc, x3=x3: e.scalar_tensor_tensor(
                    out=x3[:, oc, :], in0=ps[:, :], scalar=G1[:, oc:oc + 1], in1=x3[:, oc, :],
                    op0=ALU.mult, op1=ALU.add), [bps, self.bmod, bx], pwrites=[bx])
            P.dma("sp", xv[:, :, t0:t0 + 512], x3, reads=[bx], pwrites=[self.bX[tt]])

    def ret_core(self, qTv, kTv, ktv, vv, sgv, zTv):
        P = self.P
        pf = self.pf
        C = _consts()
        bs = self.bscr
        GC = 2
        NG = NCH // GC
        qgr = self.ring(2, [8, GC * 128], BF16, "qg")
        kgr = self.ring(2, [8, GC * 128], BF16, "kg")
        ktgr = self.ring(2, [GC, 1024], BF16, "ktg")
        vgr = self.ring(2, [GC, 2048], BF16, "vg")
        sggr = self.ring(2, [GC, 2048], BF16, "sgg")
        St = [(self.alloc([2, 512], F32), Buf("St%d" % h)) for h in range(H)]
        sbr = self.ring(8, [2, 512], BF16, "sb")
        rfbr = [self.ring(2, [2, 512], BF16, "rfb%d" % h) for h in range(H)]
        zTr = self.ring(2, [16, 512], BF16, "zTs")
        ptr = self.ring(4, [128], BF16, "pt")
        qfr = self.ring(4, [2, 128], BF16, "qf")
        qbr = self.ring(4, [2, 128], BF16, "qb")
        kdr = self.ring(4, [256], BF16, "kd")
        ynr = self.ring(4, [512], F32, "yn")
        zr = self.ring(8, [512], BF16, "z")
        str_ = self.ring(8, [16], F32, "st")
        MT = [pf[:, OMT + h * 128:OMT + (h + 1) * 128] for h in range(H)]
        DFb = [pf[:, OQD + h * 128:OQD + (h + 1) * 128].unsqueeze(1).to_broadcast([128, 2, 128]) for h in range(H)]
        DBb = [pf[:, OQD + 512 + h * 128:OQD + 512 + (h + 1) * 128].unsqueeze(1).to_broadcast([128, 2, 128])
               for h in range(H)]
        KF = [pf[:, OKD + h:OKD + h + 1] for h in range(H)]
        KB = [pf[:, OKD + 4 + h:OKD + 5 + h] for h in range(H)]
        for h in range(H):
            P.op("dve", lambda e, h=h: e.memset(St[h][0].rearrange("p a b -> p (a b)"), 0.0), [], [St[h][1]])
        def s1_load(g):
            ktg, bktg = ktgr.next()
            vg, bvg = vgr.next()
            P.dma("sp", ktg, ktv[:, g * GC:(g + 1) * GC, :], reads=[bs["ktok"]], writes=[bktg])
            P.dma("sp", vg, vv[:, g * GC:(g + 1) * GC, :], reads=[bs["v"]], writes=[bvg])
            return ktg, bktg, vg, bvg

        nxt = s1_load(NG - 1)
        for g in range(NG - 1, -1, -1):
            ktg, bktg, vg, bvg = nxt
            if g > 0:
                nxt = s1_load(g - 1)
            for c in range(GC - 1, -1, -1):
                n = g * GC + c
                if n == 0:
                    continue
                kbs = []
                for h in range(H):
                    kb, bkb = kdr.next()
                    kbs.append((kb, bkb))
                    P.op("dve", lambda e, kb=kb, h=h, c=c, ktg=ktg: e.tensor_scalar(
                        out=kb, in0=ktg[:, c, h * 256:(h + 1) * 256], scalar1=KB[h], scalar2=None, op0=ALU.mult),
                        [bktg, self.bpf], [bkb])
                pss = []
                for h in range(H):
                    kb, bkb = kbs[h]
                    for dc in range(2):
                        ps, bps = self.next_ps()
                        pss.append((ps, bps))
                        P.op("pe", lambda e, ps=ps, kb=kb, dc=dc, c=c, h=h, vg=vg: e.matmul(
                            ps[:, :], lhsT=kb[:, dc * 128:(dc + 1) * 128], rhs=vg[:, c, h * 512:(h + 1) * 512],
                            start=True, stop=True), [bkb, bvg], [bps])
                for h in range(H):
                    Sh, bSh = St[h]
                    for dc in range(2):
                        ps, bps = pss[h * 2 + dc]
                        P.op("dve", lambda e, ps=ps, dc=dc, Sh=Sh, cd=C["cdb"][h]: e.scalar_tensor_tensor(
                            out=Sh[:, dc, :], in0=Sh[:, dc, :], scalar=cd, in1=ps[:, :], op0=ALU.mult, op1=ALU.add),
                            [bps, bSh], pwrites=[bSh])
                for h in range(H):
                    Sh, bSh = St[h]
                    sbb, bsbb = sbr.next()
                    P.op("act", lambda e, sbb=sbb, Sh=Sh: e.activation(out=sbb, in_=Sh, func=AF.Identity), [bSh], [bsbb])
                    P.dma("sp", self.RBd[h, n].rearrange("(dc p) v -> p dc v", p=128), sbb, reads=[bsbb],
                          pwrites=[bs["RB"]])
        for h in range(H):
            P.op("dve", lambda e, h=h: e.memset(St[h][0].rearrange("p a b -> p (a b)"), 0.0), [], [St[h][1]])
        rfb = [None] * H
        pend = []
        self.ucnt = 0
        zTs, bz = None, None
        def s2_load(g):
            qg, bqg = qgr.next()
            kg, bkg = kgr.next()
            ktg, bktg = ktgr.next()
            vg, bvg = vgr.next()
            sgg, bsgg = sggr.next()
            tsl = slice(g * GC * 128, (g + 1) * GC * 128)
            P.dma("sp", qg, qTv[:, :, tsl], reads=[bs["qT"]], writes=[bqg])
            P.dma("sp", kg, kTv[:, :, tsl], reads=[bs["kT"]], writes=[bkg])
            P.dma("sp", ktg, ktv[:, g * GC:(g + 1) * GC, :], reads=[bs["ktok"]], writes=[bktg])
            P.dma("sp", vg, vv[:, g * GC:(g + 1) * GC, :], reads=[bs["v"]], writes=[bvg])
            P.dma("sp", sgg, sgv[:, g * GC:(g + 1) * GC, :], reads=[bs["sg"]], writes=[bsgg])
            return qg, bqg, kg, bkg, ktg, bktg, vg, bvg, sgg, bsgg

        def sbn_load(n):
            r = [None] * H
            if n < NCH - 1:
                for h in range(H):
                    t_, b_ = sbr.next()
                    r[h] = (t_, b_)
                    P.dma("sp", t_, self.RBd[h, n + 1].rearrange("(dc p) v -> p dc v", p=128),
                          reads=[bs["RB"]], writes=[b_])
            return r

        nxtg = s2_load(0)
        nxts = sbn_load(0)
        for g in range(NG):
            qg, bqg, kg, bkg, ktg, bktg, vg, bvg, sgg, bsgg = nxtg
            if g + 1 < NG:
                nxtg = s2_load(g + 1)
            for c in range(GC):
                n = g * GC + c
                if n % 4 == 0:
                    zTs, bz = zTr.next()
                sl = slice(c * 128, (c + 1) * 128)
                first, last = (n == 0), (n == NCH - 1)
                sbn = nxts
                if n + 1 < NCH:
                    nxts = sbn_load(n + 1)
                psS, bpS = self.ps[0], self.psb[0]
                for h in range(H):
                    for dc in range(2):
                        P.op("pe", lambda e, h=h, dc=dc, sl=sl, kg=kg, qg=qg: e.matmul(
                            psS[:, h * 128:(h + 1) * 128], lhsT=kg[:, 2 * h + dc, sl], rhs=qg[:, 2 * h + dc, sl],
                            start=(dc == 0), stop=(dc == 1)), [bkg, bqg], pwrites=[bpS])
                PTs, qfs, qbs, kfs = [], [], [], []
                for h in range(H):
                    PT, bPT = ptr.next()
                    PTs.append((PT, bPT))
                    P.op("dve", lambda e, PT=PT, h=h: e.tensor_tensor(
                        out=PT, in0=psS[:, h * 128:(h + 1) * 128], in1=MT[h], op=ALU.mult), [bpS, self.bpf], [bPT])
                for h in range(H):
                    if not first:
                        qf, bqf = qfr.next()
                        qfs.append((qf, bqf))
                        P.op("pool", lambda e, qf=qf, h=h, sl=sl, qg=qg: e.tensor_tensor(
                            out=qf, in0=qg[:, 2 * h:2 * h + 2, sl], in1=DFb[h], op=ALU.mult), [bqg, self.bpf], [bqf])
                    if not last:
                        qb, bqb = qbr.next()
                        qbs.append((qb, bqb))
                        P.op("pool", lambda e, qb=qb, h=h, sl=sl, qg=qg: e.tensor_tensor(
                            out=qb, in0=qg[:, 2 * h:2 * h + 2, sl], in1=DBb[h], op=ALU.mult), [bqg, self.bpf], [bqb])
                        kf, bkf = kdr.next()
                        kfs.append((kf, bkf))
                        P.op("act", lambda e, kf=kf, h=h, c=c, ktg=ktg: e.activation(
                            out=kf, in_=ktg[:, c, h * 256:(h + 1) * 256], func=AF.Identity, scale=KF[h]),
                            [bktg, self.bpf], [bkf])
                for pd in pend:
                    self._ret_transpose(pd)
                pend = []
                prev = None
                for h in range(H):
                    PT, bPT = PTs[h]
                    mm = [(PT, vg[:, c, h * 512:(h + 1) * 512], [bPT, bvg])]
                    if not last:
                        qb, bqb = qbs[h]
                        t_, b_ = sbn[h]
                        for dc in range(2):
                            mm.append((qb[:, dc, :], t_[:, dc, :], [bqb, b_]))
                    if not first:
                        qf, bqf = qfs[h]
                        r_, br_ = rfb[h]
                        for dc in range(2):
                            mm.append((qf[:, dc, :], r_[:, dc, :], [bqf, br_]))
                    psY, bpY = self.ps[1 + h], self.psb[1 + h]
                    for i, (l_, r2, rd_) in enumerate(mm):
                        P.op("pe", lambda e, psY=psY, l_=l_, r2=r2, i=i, lst=(i == len(mm) - 1): e.matmul(
                            psY[:, :], lhsT=l_, rhs=r2, start=(i == 0), stop=lst), rd_, pwrites=[bpY])
                    psU = []
                    if not last:
                        kf, bkf = kfs[h]
                        for dc in range(2):
                            ub = 5 + (self.ucnt % 3)
                            self.ucnt += 1
                            ps, bps = self.ps[ub], self.psb[ub]
                            psU.append((ps, bps))
                            P.op("pe", lambda e, ps=ps, kf=kf, dc=dc, c=c, h=h, vg=vg: e.matmul(
                                ps[:, :], lhsT=kf[:, dc * 128:(dc + 1) * 128], rhs=vg[:, c, h * 512:(h + 1) * 512],
                                start=True, stop=True), [bkf, bvg], [bps])
                    st_, bst = str_.next()
                    P.op("dve", lambda e, st_=st_, psY=psY: e.bn_stats(out=st_[:, 0:6], in_=psY[:, :]), [bpY], [bst])
                    P.op("dve", lambda e, st_=st_: e.bn_aggr(out=st_[:, 8:10], in_=st_[:, 0:6]), [bst], [bst])
                    P.op("act", lambda e, st_=st_: e.activation(out=st_[:, 10:11], in_=st_[:, 9:10], func=AF.Sqrt,
                                                               bias=self.eps, scale=1.0), [bst, self.bpf], [bst])
                    if prev is not None:
                        self._ret_gate_act(prev)
                    if not last:
                        Sh, bSh = St[h]
                        for dc in range(2):
                            ps, bps = psU[dc]
                            P.op("dve", lambda e, ps=ps, dc=dc, Sh=Sh, cd=C["cdf"][h]: e.scalar_tensor_tensor(
                                out=Sh[:, dc, :], in0=Sh[:, dc, :], scalar=cd, in1=ps[:, :], op0=ALU.mult, op1=ALU.add),
                                [bps, bSh], pwrites=[bSh])
                        r_, br_ = rfbr[h].next()
                        rfb[h] = (r_, br_)
                        P.op("act", lambda e, r_=r_, Sh=Sh: e.activation(out=r_, in_=Sh, func=AF.Identity), [bSh], [br_])
                    if prev is not None:
                        self._ret_gate_dve(prev)
                    P.op("dve", lambda e, st_=st_: e.reciprocal(out=st_[:, 11:12], in_=st_[:, 10:11]), [bst], [bst])
                    P.op("dve", lambda e, st_=st_: e.scalar_tensor_tensor(
                        out=st_[:, 12:13], in0=st_[:, 8:9], scalar=-1.0, in1=st_[:, 11:12], op0=ALU.mult, op1=ALU.mult),
                        [bst], [bst])
                    z, bzz = zr.next()
                    yn, byn = ynr.next()
                    prev = (st_, bst, psY, bpY, yn, byn, z, bzz, sgg, bsgg, c, h)
                    pend.append((z, bzz, n, h, zTs, bz, 1 + h))
                self._ret_gate_act(prev)
                self._ret_gate_dve(prev)
                if n % 4 == 3:
                    for pd in pend:
                        self._ret_transpose(pd)
                    pend = []
                    t0 = (n - 3) * 128
                    P.dma("sp", zTv[:, :, t0:t0 + 512], zTs, reads=[bz], pwrites=[bs["zT"]])

    def _ret_gate_act(self, a):
        P = self.P
        st_, bst, psY, bpY, yn, byn, z, bzz, sgg, bsgg, c, h = a
        P.op("act", lambda e: e.activation(
            out=yn, in_=psY[:, :], func=AF.Identity, bias=st_[:, 12:13], scale=st_[:, 11:12]), [bpY, bst], [byn])

    def _ret_gate_dve(self, a):
        P = self.P
        st_, bst, psY, bpY, yn, byn, z, bzz, sgg, bsgg, c, h = a
        P.op("dve", lambda e: e.tensor_tensor(
            out=z, in0=yn, in1=sgg[:, c, h * 512:(h + 1) * 512], op=ALU.mult), [byn, bsgg], [bzz])

    def _ret_transpose(self, pd):
        P = self.P
        z, bzz, n, h, zTs, bz, bank = pd
        ps, bps = self.ps[bank], self.psb[bank]
        psT = ps[:, :].bitcast(BF16)
        for vc in range(4):
            P.op("pe", lambda e, psT=psT, z=z, vc=vc: e.transpose(
                psT[:, vc * 128:(vc + 1) * 128], z[:, vc * 128:(vc + 1) * 128], self.ident),
                [bzz, self.bpb], pwrites=[bps])
        tl = (n % 4) * 128
        P.op("act", lambda e, psT=psT, tl=tl, h=h, zTs=zTs: e.activation(
            out=zTs[:, 4 * h:4 * h + 4, tl:tl + 128], in_=psT[:, 0:512].rearrange("p (a b) -> p a b", a=4),
            func=AF.Identity), [bps], pwrites=[bz])

    def final_norm(self):
        P = self.P
        Af, Bf = self.modcol(NL, "AF"), self.modcol(NL, "BF")
        xv = (self.xres if self.nsub > 0 else self.xin).rearrange("(kc p) t -> p kc t", p=128)
        ov = self.out.rearrange("(kc p) t -> p kc t", p=128)
        self.norm_setup()
        xring = self.ring(2, [8, 512], F32, "x")
        oring = self.ring(2, [8, 512], F32, "o")
        def fin_load(tt):
            x3, bx = xring.next()
            P.dma("sp", x3, xv[:, :, tt * 512:(tt + 1) * 512], reads=[self.bX[tt]], writes=[bx])
            return x3, bx

        nxt = fin_load(0)
        for tt in range(8):
            t0 = tt * 512
            x3, bx = nxt
            if tt + 1 < 8:
                nxt = fin_load(tt + 1)
            o3, bo = oring.next()
            self.norm_block(x3, bx, 0, Af, Bf, o3, bo)
            P.dma("sp", ov[:, :, t0:t0 + 512], o3, reads=[bo], pwrites=[self.bOut])


_NC_CACHE = {}


def _col(v):
    return np.ascontiguousarray(np.asarray(v, dtype=np.float32).reshape(-1, 128).T)


def kernel(x, c, w_ada, b_ada, norm_mix_g, norm_ffn_g, w_fourier_out, w_ret_in, w_ret_out,
           w_ffn_in, w_ffn_out, final_norm_g, w_ada_final, b_ada_final, _nsub=8, _final=True, _cores=None):
    C = _consts()
    key = (_nsub, _final)
    if key not in _NC_CACHE:
        _NC_CACHE[key] = Builder(_nsub, _final).build()
    nc = _NC_CACHE[key]
    f32 = lambda a: np.ascontiguousarray(np.asarray(a, dtype=np.float32))
    shared = {
        "smallb": C["smallb"], "twtab": C["twtab"], "rot": C["rot"],
        "w_ada": f32(w_ada), "w_ada_final": f32(w_ada_final), "w_fourier_out": f32(w_fourier_out),
        "w_ret_in": f32(w_ret_in), "w_ret_out": f32(w_ret_out), "w_ffn_in": f32(w_ffn_in),
        "w_ffn_out": f32(w_ffn_out),
    }
    cores = list(range(8)) if _cores is None else _cores
    in_maps = []
    for b in cores:
        sf = C["sf"].copy()
        sf[:, OC:OC + 8] = _col(c[b])
        for li in range(NL):
            sf[:, OGM + li * 8:OGM + li * 8 + 8] = _col(norm_mix_g[li])
            sf[:, OGF + li * 8:OGF + li * 8 + 8] = _col(norm_ffn_g[li])
            sf[:, OBA + li * 48:OBA + (li + 1) * 48] = _col(b_ada[li])
        sf[:, OGN:OGN + 8] = _col(final_norm_g)
        sf[:, OBF:OBF + 16] = _col(b_ada_final)
        m = dict(shared)
        m["smallf"] = sf
        m["xT"] = np.ascontiguousarray(np.asarray(x[b], dtype=np.float32).T)
        in_maps.append(m)
    res = run_bass_kernel_spmd(nc, in_maps, core_ids=list(range(len(cores))))
    import os
    if os.environ.get("KDBGZ"):
        return res.results[0]
    out = np.empty((len(cores), S, D), dtype=np.float32)
    for i in range(len(cores)):
        out[i] = res.results[i]["outT"].T
    return out
```

```python
import contextlib
import numpy as np
import ml_dtypes
import concourse.bass as bass
import concourse.mybir as mybir
from concourse.bass_utils import run_bass_kernel_spmd

F32 = mybir.dt.float32
BF16 = mybir.dt.bfloat16
ALU = mybir.AluOpType
AF = mybir.ActivationFunctionType
ENGS = ("pe", "act", "dve", "pool", "sp")

S, D, FF, NL = 4096, 1024, 2816, 4
H, DK, DV, CH = 4, 256, 512, 128
NCH = S // CH
EPS = 1e-6


class Buf:
    __slots__ = ("name", "ws", "rd", "prd")

    def __init__(self, name=""):
        self.name = name
        self.ws = []
        self.rd = []
        self.prd = []


class Op:
    __slots__ = ("eng", "fn", "reads", "writes", "pwrites", "dma", "deps", "sig", "sigval", "dsem",
                 "dval", "dprev", "eidx", "barrier")

    def __init__(self, eng, fn, reads, writes, pwrites, dma, barrier=False):
        self.eng = eng
        self.fn = fn
        self.reads = reads
        self.writes = writes
        self.pwrites = pwrites
        self.dma = dma
        self.barrier = barrier
        self.deps = ()
        self.sig = False
        self.sigval = 0
        self.dsem = None
        self.dval = 0
        self.dprev = 0
        self.eidx = 0


class Prog:
    def __init__(self, nc, n_dma_sems=16):
        self.nc = nc
        self.ops = []
        self.n_dma_sems = n_dma_sems

    def op(self, eng, fn, reads=(), writes=(), pwrites=(), dma=False):
        self.ops.append(Op(eng, fn, tuple(reads), tuple(writes), tuple(pwrites), dma))

    def dma(self, queue, out, in_, reads=(), writes=(), pwrites=()):
        self.op(queue, lambda e: e.dma_start(out=out, in_=in_), reads, writes, pwrites, dma=True)

    def fence(self, eng, reads):
        self.op(eng, None, reads, ())

    def barrier(self, out, in_):
        self.ops.append(Op("sp", lambda e: e.dma_start(out=out, in_=in_), (), (), (), True, barrier=True))

    def analyze(self):
        ops = self.ops
        cnt = {e: 0 for e in ENGS}
        dcnt = {e: 0 for e in ENGS}
        duse = {}
        last_barrier = None
        last_compute = {e: None for e in ENGS}
        dmas_since = []
        for i, o in enumerate(ops):
            o.eidx = cnt[o.eng]
            cnt[o.eng] += 1
            hard = set()
            war = set()
            if o.barrier:
                for e in ENGS:
                    if last_compute[e] is not None:
                        hard.add(last_compute[e])
                hard.update(dmas_since)
                dmas_since = []
                if last_barrier is not None:
                    hard.add(last_barrier)
                last_barrier = i
            else:
                if last_barrier is not None:
                    hard.add(last_barrier)
                for b in o.reads:
                    hard.update(b.ws)
                for b in o.writes:
                    hard.update(b.ws)
                    war.update(b.rd)
                    war.update(b.prd)
                for b in o.pwrites:
                    if b.rd:
                        b.prd = b.rd
                        b.rd = []
                        b.ws = []
                    war.update(b.prd)
                for b in o.reads:
                    b.rd.append(i)
                for b in o.writes:
                    b.ws = [i]
                    b.rd = []
                    b.prd = []
                for b in o.pwrites:
                    b.ws.append(i)
            war -= hard
            hard.discard(i)
            war.discard(i)
            keep = []
            for d in hard:
                od = ops[d]
                if (not od.dma) and (not o.dma) and od.eng == o.eng:
                    if o.eng == "pe":
                        continue
                    if o.eidx - od.eidx > 3:
                        continue
                keep.append(d)
            for d in war:
                od = ops[d]
                if (not od.dma) and od.eng == o.eng:
                    continue
                keep.append(d)
            best = {}
            keep2 = []
            for d in keep:
                od = ops[d]
                if od.dma:
                    keep2.append(d)
                elif best.get(od.eng, -1) < d:
                    best[od.eng] = d
            keep2.extend(best.values())
            o.deps = tuple(sorted(keep2))
            for d in o.deps:
                if not ops[d].dma:
                    ops[d].sig = True
            if o.dma:
                j = dcnt[o.eng] % self.n_dma_sems
                dcnt[o.eng] += 1
                key = (o.eng, j)
                o.dsem = key
                o.dprev = duse.get(key, 0)
                o.dval = o.dprev + 16
                duse[key] = o.dval
                dmas_since.append(i)
            elif o.fn is not None:
                last_compute[o.eng] = i
        sc = {e: 0 for e in ENGS}
        for o in ops:
            if o.sig:
                sc[o.eng] += 1
                o.sigval = sc[o.eng]

    def emit(self, stack):
        nc = self.nc
        self.analyze()
        ops = self.ops
        esem = {e: stack.enter_context(nc.semaphore("s_" + e)) for e in ENGS}
        dsem = {}
        for o in ops:
            if o.dma and o.dsem not in dsem:
                dsem[o.dsem] = stack.enter_context(nc.semaphore("d_%s%d" % o.dsem))
        block = stack.enter_context(nc.Block())
        per = {e: [o for o in ops if o.eng == e] for e in ENGS}

        def run(eng_name, engine):
            seen = {}
            for o in per[eng_name]:
                need = {}
                for d in o.deps:
                    od = ops[d]
                    if od.dma:
                        k, s, v = od.dsem, dsem[od.dsem], od.dval
                    else:
                        k, s, v = od.eng, esem[od.eng], od.sigval
                    if need.get(k, (None, 0))[1] < v:
                        need[k] = (s, v)
                if o.dma and o.dprev > 0:
                    k = o.dsem
                    if need.get(k, (None, 0))[1] < o.dprev:
                        need[k] = (dsem[k], o.dprev)
                for k, (s, v) in need.items():
                    if seen.get(k, 0) >= v:
                        continue
                    seen[k] = v
                    engine.wait_ge(s, v)
                if o.fn is None:
                    continue
                ins = o.fn(engine)
                if o.dma:
                    ins.then_inc(dsem[o.dsem], 16)
                elif o.sig:
                    ins.then_inc(esem[eng_name], 1)

        @block.tensor
        def _(e):
            run("pe", e)

        @block.scalar
        def _(e):
            run("act", e)

        @block.vector
        def _(e):
            run("dve", e)

        @block.gpsimd
        def _(e):
            run("pool", e)

        @block.sync
        def _(e):
            run("sp", e)


class Ring:
    def __init__(self, items):
        self.items = items
        self.i = 0

    def next(self):
        it = self.items[self.i % len(self.items)]
        self.i += 1
        return it


OC, OGM, OGF, OGN, OBA, OBF, OEPS, OKD, OMT, OQD = 0, 8, 40, 72, 80, 272, 288, 289, 297, 809
NFCOL = OQD + 1024
OMOD = NFCOL
NPF = OMOD + 5 * 64
OID, OON, OCT = 0, 128, 256
OW1 = OCT + 1024
NPB = OW1 + 64

_CONST = {}


def _consts():
    if _CONST:
        return _CONST
    bf = ml_dtypes.bfloat16
    p = np.arange(128)
    r_ = np.arange(32)
    q_ = np.arange(128)
    m_ = (p[:, None, None].astype(np.int64) * (32 * q_[None, None, :] + r_[None, :, None])) % 4096
    ang = m_.astype(np.float64) * (2.0 * np.pi / 4096.0)
    tw = np.stack([np.cos(ang) / 8.0, -np.sin(ang) / 8.0], axis=1)
    _CONST["twtab"] = np.ascontiguousarray(tw.reshape(128, 8192).astype(bf))
    a_ = np.arange(32)
    th = ((a_[:, None] * r_[None, :]) % 32).astype(np.float64) * (2.0 * np.pi / 32.0)
    w1 = np.zeros((64, 64), dtype=np.float64)
    w1[0:32, 0:32] = np.cos(th)
    w1[32:64, 0:32] = -np.sin(th)
    w1[0:32, 32:64] = np.sin(th)
    w1[32:64, 32:64] = np.cos(th)
    w1 /= 8.0
    cin = (np.arange(2)[None, :, None] * 128 + p[:, None, None]).astype(np.int64)
    cout = np.arange(256)[None, None, :]
    ang = ((cin * cout) % 256).astype(np.float64) * (2.0 * np.pi / 256.0)
    ct = np.stack([np.cos(ang) / 16.0, np.sin(ang) / 16.0], axis=1)
    sb = np.zeros((128, NPB), dtype=bf)
    sb[:, OID:OID + 128] = np.eye(128).astype(bf)
    sb[:, OON:OON + 128] = np.full((128, 128), 1.0 / 1024.0).astype(bf)
    sb[:, OCT:OCT + 1024] = ct.reshape(128, 1024).astype(bf)
    sb[0:64, OW1:OW1 + 64] = w1.astype(bf)
    _CONST["smallb"] = sb
    inv = 10000.0 ** (-np.arange(128, dtype=np.float64) / 128.0)
    ang = inv[:, None] * np.arange(S, dtype=np.float64)[None, :]
    _CONST["rot"] = np.stack([np.cos(ang), np.sin(ang)], axis=1).astype(np.float32)
    hidx = np.arange(H, dtype=np.float64)
    lgf = np.log1p(-np.exp2(-5.0 - hidx))
    lgb = lgf[::-1].copy()
    sf = np.zeros((128, NFCOL), dtype=np.float32)
    sf[:, OEPS] = EPS
    s_ = np.arange(CH, dtype=np.float64)
    for h in range(H):
        sf[:, OKD + h] = np.exp((CH - 1.0 - s_) * lgf[h])
        sf[:, OKD + 4 + h] = np.exp(s_ * lgb[h])
        diff = s_[None, :] - s_[:, None]
        mt = np.where(diff >= 0, np.exp(np.where(diff >= 0, diff, 0) * lgf[h]),
                      np.exp(np.where(diff < 0, -diff, 0) * lgb[h]))
        sf[:, OMT + h * 128:OMT + (h + 1) * 128] = mt
        sf[:, OQD + h * 128:OQD + (h + 1) * 128] = np.exp((s_ + 1.0) * lgf[h])[None, :]
        sf[:, OQD + 512 + h * 128:OQD + 512 + (h + 1) * 128] = np.exp((CH - s_) * lgb[h])[None, :]
    _CONST["sf"] = sf
    _CONST["cdf"] = [float(np.exp(CH * lgf[h])) for h in range(H)]
    _CONST["cdb"] = [float(np.exp(CH * lgb[h])) for h in range(H)]
    return _CONST


ARENA_WORDS = 49152


class Builder:
    def __init__(self, nsub=8, final=True):
        self.nsub = nsub
        self.final = final
        self.nc = nc = bass.Bass("TRN2", target_bir_lowering=False)
        din = lambda n, s, dt: nc.dram_tensor(n, s, dt, kind="ExternalInput").ap()
        dsc = lambda n, s, dt: nc.dram_tensor(n, s, dt, kind="Internal").ap()
        self.xin = din("xT", [D, S], F32)
        self.smallf = din("smallf", [128, NFCOL], F32)
        self.smallb = din("smallb", [128, NPB], BF16)
        self.twtab = din("twtab", [128, 8192], BF16)
        self.rot = din("rot", [128, 2, S], F32)
        self.w_ada = din("w_ada", [NL, D, 6 * D], F32)
        self.w_ada_final = din("w_ada_final", [D, 2 * D], F32)
        self.w_four = din("w_fourier_out", [2, D, D], F32)
        self.w_ret_in = din("w_ret_in", [2, D, 6 * D], F32)
        self.w_ret_out = din("w_ret_out", [2, 2 * D, D], F32)
        self.w_ffn_in = din("w_ffn_in", [NL, D, 2 * FF], F32)
        self.w_ffn_out = din("w_ffn_out", [NL, FF, D], F32)
        self.out = nc.dram_tensor("outT", [D, S], F32, kind="ExternalOutput").ap()
        self.xres = dsc("xres", [D, S], F32)
        self.Gd = [dsc("Gc", [S, D], BF16), dsc("Gs", [S, D], BF16)]
        self.Bd = dsc("Bd", [2, 32, 128, D], BF16)
        self.hTd = dsc("hTd", [D, S], BF16)
        self.qTd = dsc("qTd", [D, S], BF16)
        self.kTd = dsc("kTd", [D, S], BF16)
        self.ktokd = dsc("ktokd", [S, D], BF16)
        self.vd = dsc("vd", [S, 2 * D], BF16)
        self.sgd = dsc("sgd", [S, 2 * D], BF16)
        self.RBd = dsc("RBd", [H, NCH, DK, DV], BF16)
        self.zTd = dsc("zTd", [2 * D, S], BF16)
        self.dmy = dsc("dmy", [2, 64], F32)
        import os
        self.dbgz = None
        if os.environ.get("KDBGZ"):
            self.dbgz = nc.dram_tensor("dbgz", [2 * D, S], BF16, kind="ExternalOutput").ap()
            self.dbgq = nc.dram_tensor("dbgq", [D, S], BF16, kind="ExternalOutput").ap()
            self.dbgr = nc.dram_tensor("dbgr", [H, NCH, DK, DV], BF16, kind="ExternalOutput").ap()
        self.bX = [Buf("x%d" % i) for i in range(8)]
        self.bOut = Buf("out")
        self.bscr = {n: Buf(n) for n in ("G", "B", "hT", "qT", "kT", "ktok", "v", "sg", "RB", "zT")}

    def reset(self):
        self.aoff = 0

    def alloc(self, free_shape, dt):
        n = 1
        for s_ in free_shape:
            n *= s_
        nw = (n * (2 if dt == BF16 else 4) + 3) // 4
        assert self.aoff + nw <= ARENA_WORDS, ("arena overflow", self.aoff, nw)
        a = self.arena[:, self.aoff:self.aoff + nw]
        self.aoff += nw
        if dt != F32:
            a = a.bitcast(dt)
        if len(free_shape) == 2:
            a = a.rearrange("p (a b) -> p a b", a=free_shape[0])
        elif len(free_shape) == 3:
            a = a.rearrange("p (a b c) -> p a b c", a=free_shape[0], b=free_shape[1])
        return a

    def ring(self, n, free_shape, dt, name="r"):
        return Ring([(self.alloc(free_shape, dt), Buf(name + str(i))) for i in range(n)])

    def next_ps(self):
        i = self.psi % 8
        self.psi += 1
        return self.ps[i], self.psb[i]

    def phase_barrier(self):
        self.P.barrier(self.dmy[1:2, :], self.smallf[0:1, 0:64])
        self.reset()

    def modcol(self, li, which):
        base = OMOD + li * 64
        off = {"B1": 0, "G1": 16, "B2": 24, "G2": 40, "A1": 48, "A2": 56, "BF": 0, "AF": 48}[which]
        return self.pf[:, base + off:base + off + 8]

    def build(self):
        nc = self.nc
        with contextlib.ExitStack() as st:
            self.arena = st.enter_context(nc.sbuf_tensor("arena", [128, ARENA_WORDS], F32))
            self.pf = st.enter_context(nc.sbuf_tensor("pf", [128, NPF], F32))
            self.pb = st.enter_context(nc.sbuf_tensor("pb", [128, NPB], BF16))
            self.ps = [st.enter_context(nc.psum_tensor("ps%d" % i, [128, 512], F32)) for i in range(8)]
            self.psb = [Buf("ps%d" % i) for i in range(8)]
            self.psi = 0
            self.bpf = Buf("pf")
            self.bpb = Buf("pb")
            self.bmod = Buf("mod")
            self.P = P = Prog(nc)
            self.reset()
            P.dma("sp", self.pf[:, 0:NFCOL], self.smallf[:, :], writes=[self.bpf])
            P.dma("sp", self.pb[:, :], self.smallb[:, :], writes=[self.bpb])
            self.ident = self.pb[:, OID:OID + 128]
            self.onesD = self.pb[:, OON:OON + 128]
            self.eps = self.pf[:, OEPS:OEPS + 1]
            self.adaln()
            subs = []
            for li in range(NL):
                subs.append(("mix", li))
                subs.append(("ffn", li))
            for kind, li in subs[:self.nsub]:
                self.phase_barrier()
                xs = self.xin if (kind == "mix" and li == 0) else self.xres
                if kind == "ffn":
                    self.ffn(li)
                elif li % 2 == 0:
                    self.fourier(li, xs)
                else:
                    self.retention(li, xs)
            self.phase_barrier()
            if self.final:
                self.final_norm()
            else:
                src = self.xres if self.nsub > 0 else self.xin
                P.dma("sp", self.out[:, :], src[:, :], reads=self.bX, pwrites=[self.bOut])
            if self.dbgz is not None:
                P.dma("sp", self.dbgz[:, :], self.zTd[:, :], pwrites=[self.bOut])
                P.dma("sp", self.dbgq[:, :], self.qTd[:, :], pwrites=[self.bOut])
                for h_ in range(H):
                    P.dma("sp", self.dbgr[h_].rearrange("n d v -> n (d v)"), self.RBd[h_].rearrange("n d v -> n (d v)"), pwrites=[self.bOut])
            P.fence("sp", [self.bOut])
            P.emit(st)
        return nc

    def adaln(self):
        P = self.P
        pf = self.pf
        cact = self.alloc([8], F32)
        cactb = self.alloc([8], BF16)
        bca, bcb = Buf("cact"), Buf("cactb")
        P.op("act", lambda e: e.activation(out=cact, in_=pf[:, OC:OC + 8], func=AF.Silu), [self.bpf], [bca])
        P.op("dve", lambda e: e.tensor_copy(out=cactb, in_=cact), [bca], [bcb])
        wring = self.ring(2, [8, 1536], BF16, "wada")
        for li in range(NL + 1):
            ncols = 6 * D if li < NL else 2 * D
            qw = 1536 if li < NL else 1024
            src = (self.w_ada[li] if li < NL else self.w_ada_final).rearrange("(kc p) n -> p kc n", p=128)
            ps, bps = self.next_ps()
            for q in range(ncols // qw):
                wt, bw = wring.next()
                P.dma("pool", wt[:, :, 0:qw], src[:, :, q * qw:(q + 1) * qw], writes=[bw])
                for jc in range(qw // 128):
                    col = q * (qw // 128) + jc
                    for kc in range(8):
                        P.op("pe", lambda e, ps=ps, wt=wt, col=col, jc=jc, kc=kc: e.matmul(
                            ps[:, col:col + 1], lhsT=wt[:, kc, jc * 128:(jc + 1) * 128], rhs=cactb[:, kc:kc + 1],
                            start=(kc == 0), stop=(kc == 7)), [bw, bcb], pwrites=[bps])
            nm = ncols // 128
            base = OMOD + li * 64
            boff = OBA + li * 48 if li < NL else OBF
            P.op("dve", lambda e, ps=ps, nm=nm, base=base, boff=boff: e.tensor_tensor(
                out=pf[:, base:base + nm], in0=ps[:, 0:nm], in1=pf[:, boff:boff + nm], op=ALU.add),
                [bps, self.bpf], pwrites=[self.bmod])
            if li < NL:
                P.op("dve", lambda e, base=base, li=li: e.scalar_tensor_tensor(
                    out=pf[:, base + 48:base + 56], in0=pf[:, base + 8:base + 16], scalar=1.0,
                    in1=pf[:, OGM + li * 8:OGM + li * 8 + 8], op0=ALU.add, op1=ALU.mult),
                    [self.bmod, self.bpf], pwrites=[self.bmod])
                P.op("dve", lambda e, base=base, li=li: e.scalar_tensor_tensor(
                    out=pf[:, base + 56:base + 64], in0=pf[:, base + 32:base + 40], scalar=1.0,
                    in1=pf[:, OGF + li * 8:OGF + li * 8 + 8], op0=ALU.add, op1=ALU.mult),
                    [self.bmod, self.bpf], pwrites=[self.bmod])
            else:
                P.op("dve", lambda e, base=base: e.scalar_tensor_tensor(
                    out=pf[:, base + 48:base + 56], in0=pf[:, base + 8:base + 16], scalar=1.0,
                    in1=pf[:, OGN:OGN + 8], op0=ALU.add, op1=ALU.mult),
                    [self.bmod, self.bpf], pwrites=[self.bmod])

    def norm_setup(self):
        self.n_sq = self.ring(1, [8, 512], BF16, "sq")
        self.n_rstd = self.ring(2, [512], F32, "rstd")
        self.n_tmp = self.ring(3, [512], F32, "ntmp")

    def norm_block(self, x3, bx, blk, A, B, out3, bout):
        P = self.P
        sl = slice(blk * 512, (blk + 1) * 512)
        sq, bsq = self.n_sq.next()
        rstd, brs = self.n_rstd.next()
        ps, bps = self.next_ps()
        P.op("act", lambda e: e.activation(out=sq, in_=x3[:, :, sl], func=AF.Square), [bx], [bsq])
        for kc in range(8):
            P.op("pe", lambda e, kc=kc: e.matmul(ps[:, :], lhsT=self.onesD, rhs=sq[:, kc, :],
                                                   start=(kc == 0), stop=(kc == 7)), [bsq, self.bpb], pwrites=[bps])
        P.op("act", lambda e: e.activation(out=rstd, in_=ps[:, :], func=AF.Sqrt, bias=self.eps, scale=1.0),
             [bps, self.bpf], [brs])
        P.op("dve", lambda e: e.reciprocal(out=rstd, in_=rstd), [brs], [brs])
        for kc in range(8):
            tmp, btmp = self.n_tmp.next()
            P.op("dve", lambda e, kc=kc, tmp=tmp: e.scalar_tensor_tensor(
                out=tmp, in0=x3[:, kc, sl], scalar=A[:, kc:kc + 1], in1=rstd, op0=ALU.mult, op1=ALU.mult),
                [bx, brs, self.bmod], [btmp])
            P.op("act", lambda e, kc=kc, tmp=tmp: e.activation(
                out=out3[:, kc, sl], in_=tmp, func=AF.Identity, bias=B[:, kc:kc + 1], scale=1.0),
                [btmp, self.bmod], pwrites=[bout])

    def ffn(self, li):
        P = self.P
        A2, B2, G2 = self.modcol(li, "A2"), self.modcol(li, "B2"), self.modcol(li, "G2")
        wi_v = self.w_ffn_in[li].rearrange("(kc p) n -> p kc n", p=128)
        wo_v = self.w_ffn_out[li].rearrange("(j p) n -> p j n", p=128)
        xv = self.xres.rearrange("(kc p) t -> p kc t", p=128)
        self.norm_setup()
        T = 1024
        NT = S // T
        xring = self.ring(1, [8, T], F32, "x")
        hring = self.ring(2, [8, T], BF16, "hT")
        act, bact = self.alloc([22, T], BF16), Buf("act")
        wgr = self.ring(3, [2, 8, 256], BF16, "wgu")
        wor = self.ring(2, [22, 256], BF16, "wo")
        sil = self.ring(2, [512], F32, "sil")
        xor_ = self.ring(3, [2, 512], F32, "xo")

        def load_norm(tt):
            t0 = tt * T
            x3, bx = xring.next()
            P.dma("sp", x3, xv[:, :, t0:t0 + T], reads=[self.bX[2 * tt], self.bX[2 * tt + 1]], writes=[bx])
            hT, bh = hring.next()
            for blk in range(T // 512):
                self.norm_block(x3, bx, blk, A2, B2, hT, bh)
            return hT, bh

        cur = load_norm(0)
        for tt in range(NT):
            t0 = tt * T
            hT, bh = cur
            xb = [self.bX[2 * tt], self.bX[2 * tt + 1]]
            for g in range(11):
                w, bw = wgr.next()
                P.dma("pool", w[:, 0], wi_v[:, :, g * 256:(g + 1) * 256], pwrites=[bw])
                P.dma("pool", w[:, 1], wi_v[:, :, FF + g * 256:FF + (g + 1) * 256], pwrites=[bw])
                for jc in range(2):
                    j = g * 2 + jc
                    for blk in range(T // 512):
                        sl = slice(blk * 512, (blk + 1) * 512)
                        psg, bpg = self.next_ps()
                        psu, bpu = self.next_ps()
                        for gu, ps, bps in ((0, psg, bpg), (1, psu, bpu)):
                            for kc in range(8):
                                P.op("pe", lambda e, w=w, gu=gu, ps=ps, kc=kc, jc=jc, sl=sl, hT=hT: e.matmul(
                                    ps[:, :], lhsT=w[:, gu, kc, jc * 128:(jc + 1) * 128], rhs=hT[:, kc, sl],
                                    start=(kc == 0), stop=(kc == 7)), [bw, bh], pwrites=[bps])
                        sg, bsg = sil.next()
                        P.op("act", lambda e, sg=sg, psg=psg: e.activation(out=sg, in_=psg[:, :], func=AF.Silu),
                             [bpg], [bsg])
                        P.op("dve", lambda e, sg=sg, psu=psu, j=j, sl=sl: e.tensor_tensor(
                            out=act[:, j, sl], in0=sg, in1=psu[:, :], op=ALU.mult), [bsg, bpu], pwrites=[bact])
            if tt + 1 < NT:
                cur = load_norm(tt + 1)
            for og in range(4):
                w, bw = wor.next()
                P.dma("pool", w, wo_v[:, :, og * 256:(og + 1) * 256], writes=[bw])
                for blk in range(T // 512):
                    sl = slice(blk * 512, (blk + 1) * 512)
                    xo, bxo = xor_.next()
                    P.dma("sp", xo, xv[:, og * 2:og * 2 + 2, t0 + blk * 512:t0 + (blk + 1) * 512], reads=xb, writes=[bxo])
                    for oc2 in range(2):
                        oc = og * 2 + oc2
                        ps, bps = self.next_ps()
                        for j in range(22):
                            P.op("pe", lambda e, w=w, ps=ps, j=j, oc2=oc2, sl=sl: e.matmul(
                                ps[:, :], lhsT=w[:, j, oc2 * 128:(oc2 + 1) * 128], rhs=act[:, j, sl],
                                start=(j == 0), stop=(j == 21)), [bw, bact], pwrites=[bps])
                        P.op("dve", lambda e, ps=ps, oc=oc, oc2=oc2, xo=xo: e.scalar_tensor_tensor(
                            out=xo[:, oc2, :], in0=ps[:, :], scalar=G2[:, oc:oc + 1], in1=xo[:, oc2, :],
                            op0=ALU.mult, op1=ALU.add), [bps, self.bmod, bxo], pwrites=[bxo])
                    P.dma("sp", xv[:, og * 2:og * 2 + 2, t0 + blk * 512:t0 + (blk + 1) * 512], xo, reads=[bxo], pwrites=xb)

    def fourier(self, li, xsrc):
        P = self.P
        j = li // 2
        A1, B1, G1 = self.modcol(li, "A1"), self.modcol(li, "B1"), self.modcol(li, "G1")
        xv_in = xsrc.rearrange("(kc p) t -> p kc t", p=128)
        xv = self.xres.rearrange("(kc p) t -> p kc t", p=128)
        chan = self.pb[:, OCT:OCT + 1024].rearrange("p (cs cc n) -> p cs cc n", cs=2, cc=2)
        Gv = [g.rearrange("(a p) f -> p a f", p=128) for g in self.Gd]
        bG = self.bscr["G"]
        self.norm_setup()
        wo, bwo = self.alloc([8, 1024], BF16), [Buf("wo0"), Buf("wo1")]
        wsrc = self.w_four[j].rearrange("(kc p) n -> p kc n", p=128)
        for half in range(2):
            P.dma("pool", wo[:, :, half * 512:(half + 1) * 512], wsrc[:, :, half * 512:(half + 1) * 512],
                  writes=[bwo[half]])
        xring = self.ring(2, [8, 512], F32, "x")
        hring = self.ring(2, [8, 512], BF16, "hT")
        Yr = [self.ring(2, [8, 512], BF16, "Yc"), self.ring(2, [8, 512], BF16, "Ys")]
        gst = [self.ring(2, [4, 1024], BF16, "gstc"), self.ring(2, [4, 1024], BF16, "gsts")]
        flip = 0
        def f1_load(tt):
            x3, bx = xring.next()
            P.dma("sp", x3, xv_in[:, :, tt * 512:(tt + 1) * 512], reads=[self.bX[tt]], writes=[bx])
            return x3, bx

        nxt = f1_load(0)
        for tt in range(8):
            t0 = tt * 512
            x3, bx = nxt
            if tt + 1 < 8:
                nxt = f1_load(tt + 1)
            hT, bh = hring.next()
            Y = [Yr[0].next(), Yr[1].next()]
            self.norm_block(x3, bx, 0, A1, B1, hT, bh)
            for cs in range(2):
                Yt, bY = Y[cs]
                for oc in range(8):
                    g, ocl = oc // 2, oc % 2
                    ps, bps = self.next_ps()
                    for cc in range(2):
                        P.op("pe", lambda e, ps=ps, cs=cs, cc=cc, ocl=ocl, g=g, hT=hT: e.matmul(
                            ps[:, :], lhsT=chan[:, cs, cc, ocl * 128:(ocl + 1) * 128], rhs=hT[:, 2 * g + cc, :],
                            start=(cc == 0), stop=(cc == 1)), [self.bpb, bh], pwrites=[bps])
                    flip ^= 1
                    if flip:
                        P.op("act", lambda e, ps=ps, Yt=Yt, oc=oc: e.activation(out=Yt[:, oc, :], in_=ps[:, :], func=AF.Identity),
                             [bps], pwrites=[bY])
                    else:
                        P.op("dve", lambda e, ps=ps, Yt=Yt, oc=oc: e.tensor_copy(out=Yt[:, oc, :], in_=ps[:, :]),
                             [bps], pwrites=[bY])
            for cs in range(2):
                Yt, bY = Y[cs]
                go, bgo = gst[cs].next()
                for s4 in range(4):
                    for half in range(2):
                        ps, bps = self.next_ps()
                        for oc in range(8):
                            P.op("pe", lambda e, ps=ps, Yt=Yt, oc=oc, s4=s4, half=half: e.matmul(
                                ps[:, :], lhsT=Yt[:, oc, s4 * 128:(s4 + 1) * 128], rhs=wo[:, oc, half * 512:(half + 1) * 512],
                                start=(oc == 0), stop=(oc == 7)), [bY, bwo[half]], pwrites=[bps])
                        flip ^= 1
                        if flip:
                            P.op("act", lambda e, ps=ps, go=go, s4=s4, half=half: e.activation(
                                out=go[:, s4, half * 512:(half + 1) * 512], in_=ps[:, :], func=AF.Identity), [bps], pwrites=[bgo])
                        else:
                            P.op("dve", lambda e, ps=ps, go=go, s4=s4, half=half: e.tensor_copy(
                                out=go[:, s4, half * 512:(half + 1) * 512], in_=ps[:, :]), [bps], pwrites=[bgo])
                P.dma("sp", Gv[cs][:, tt * 4:(tt + 1) * 4, :], go, reads=[bgo], pwrites=[bG])
        self.phase_barrier()
        bB = self.bscr["B"]
        PB = 8
        W1 = self.pb[0:64, OW1:OW1 + 64]
        Gap = [g.rearrange("(a p) f -> a p f", p=128) for g in self.Gd]
        Bst_v = self.Bd.rearrange("c r p f -> (c r) p f")
        Bld_v = self.Bd.rearrange("c r p f -> p c r f")
        tw, btw = self.alloc([2, 32, 128], BF16), Buf("tw")
        P.dma("sp", tw.rearrange("p a b c -> p (a b c)"), self.twtab[:, :], writes=[btw])
        yir = self.ring(2, [PB, 1024], BF16, "yin")
        bor = self.ring(2, [PB, 1024], BF16, "bst")
        binr = self.ring(2, [2, 32, 128], BF16, "bin")
        xfr = self.ring(2, [S], F32, "xf")
        flip = 0
        def f2a_load(pb_):
            yi, byi = yir.next()
            P.dma("sp", yi[0:32], Gap[0][:, pb_ * PB:(pb_ + 1) * PB, :], reads=[bG], pwrites=[byi])
            P.dma("sp", yi[32:64], Gap[1][:, pb_ * PB:(pb_ + 1) * PB, :], reads=[bG], pwrites=[byi])
            return yi, byi

        nxt = f2a_load(0)
        for pb_ in range(128 // PB):
            yi, byi = nxt
            if pb_ + 1 < 128 // PB:
                nxt = f2a_load(pb_ + 1)
            bo, bbo = bor.next()
            for pi in range(PB):
                for half in range(2):
                    ps, bps = self.next_ps()
                    P.op("pe", lambda e, ps=ps, yi=yi, pi=pi, half=half: e.matmul(
                        ps[0:64, :], lhsT=W1, rhs=yi[0:64, pi, half * 512:(half + 1) * 512], start=True, stop=True),
                        [self.bpb, byi], [bps])
                    flip ^= 1
                    if flip:
                        P.op("act", lambda e, ps=ps, bo=bo, pi=pi, half=half: e.activation(
                            out=bo[0:64, pi, half * 512:(half + 1) * 512], in_=ps[0:64, :], func=AF.Identity),
                            [bps], pwrites=[bbo])
                    else:
                        P.op("dve", lambda e, ps=ps, bo=bo, pi=pi, half=half: e.tensor_copy(
                            out=bo[0:64, pi, half * 512:(half + 1) * 512], in_=ps[0:64, :]), [bps], pwrites=[bbo])
            P.dma("sp", Bst_v[:, pb_ * PB:(pb_ + 1) * PB, :], bo[0:64], reads=[bbo], pwrites=[bB])
        def f2b_load(fc):
            bi, bbi = binr.next()
            P.dma("sp", bi, Bld_v[:, :, :, fc * 128:(fc + 1) * 128], reads=[bB], writes=[bbi])
            xf, bxf = xfr.next()
            P.dma("sp", xf, xv_in[:, fc, :], reads=self.bX, writes=[bxf])
            return bi, bbi, xf, bxf

        nxt = f2b_load(0)
        for fc in range(8):
            bi, bbi, xf, bxf = nxt
            if fc + 1 < 8:
                nxt = f2b_load(fc + 1)
            xf3 = xf.rearrange("p (q r) -> p r q", r=32)
            for rg in range(8):
                ps, bps = self.next_ps()
                for r4 in range(4):
                    r = rg * 4 + r4
                    for cs in range(2):
                        P.op("pe", lambda e, ps=ps, bi=bi, r=r, r4=r4, cs=cs: e.matmul(
                            ps[:, r4 * 128:(r4 + 1) * 128], lhsT=bi[:, cs, r, :], rhs=tw[:, cs, r, :],
                            start=(cs == 0), stop=(cs == 1)), [bbi, btw], pwrites=[bps])
                P.op("dve", lambda e, ps=ps, xf3=xf3, rg=rg, fc=fc: e.scalar_tensor_tensor(
                    out=xf3[:, rg * 4:(rg + 1) * 4, :], in0=ps[:, :].rearrange("p (a b) -> p a b", a=4),
                    scalar=G1[:, fc:fc + 1], in1=xf3[:, rg * 4:(rg + 1) * 4, :], op0=ALU.mult, op1=ALU.add),
                    [bps, self.bmod, bxf], pwrites=[bxf])
            P.dma("sp", xv[:, fc, :], xf, reads=[bxf], pwrites=self.bX)

    def retention(self, li, xsrc):
        P = self.P
        pf = self.pf
        j = li // 2
        C = _consts()
        A1, B1, G1 = self.modcol(li, "A1"), self.modcol(li, "B1"), self.modcol(li, "G1")
        xv = self.xres.rearrange("(kc p) t -> p kc t", p=128)
        win = self.w_ret_in[j].rearrange("(kc p) n -> p kc n", p=128)
        hTv = self.hTd.rearrange("(kc p) t -> p kc t", p=128)
        qTv = self.qTd.rearrange("(kc p) t -> p kc t", p=128)
        kTv = self.kTd.rearrange("(kc p) t -> p kc t", p=128)
        ktv = self.ktokd.rearrange("(a p) f -> p a f", p=128)
        vv = self.vd.rearrange("(a p) f -> p a f", p=128)
        sgv = self.sgd.rearrange("(a p) f -> p a f", p=128)
        zTv = self.zTd.rearrange("(c p) t -> p c t", p=128)
        bs = self.bscr
        self.norm_setup()
        wq, bwq = self.alloc([8, 1024], BF16), [Buf("wq%d" % i) for i in range(H)]
        wk, bwk = self.alloc([8, 1024], BF16), [Buf("wk%d" % i) for i in range(H)]
        for h_ in range(H):
            P.dma("pool", wq[:, :, h_ * 256:(h_ + 1) * 256], win[:, :, h_ * 256:(h_ + 1) * 256], writes=[bwq[h_]])
        for h_ in range(H):
            P.dma("pool", wk[:, :, h_ * 256:(h_ + 1) * 256], win[:, :, 1024 + h_ * 256:1024 + (h_ + 1) * 256],
                  writes=[bwk[h_]])
        xring = self.ring(2, [8, 512], F32, "x")
        hring = self.ring(2, [8, 512], BF16, "hT")
        rotr = self.ring(2, [2, 512], F32, "rot")
        x12 = self.ring(3, [2, 512], F32, "x12")
        tq = self.ring(3, [4, 512], F32, "tq")
        qor = self.ring(2, [8, 512], BF16, "qo")
        kor = self.ring(2, [8, 512], BF16, "ko")
        ktr = self.ring(2, [4, 1024], BF16, "kt")
        def r1a_load(tt):
            x3, bx = xring.next()
            P.dma("sp", x3, xv[:, :, tt * 512:(tt + 1) * 512], reads=[self.bX[tt]], writes=[bx])
            rt, brt = rotr.next()
            P.dma("sp", rt, self.rot[:, :, tt * 512:(tt + 1) * 512], writes=[brt])
            return x3, bx, rt, brt

        nxt = r1a_load(0)
        for tt in range(8):
            t0 = tt * 512
            x3, bx, rt, brt = nxt
            if tt + 1 < 8:
                nxt = r1a_load(tt + 1)
            hT, bh = hring.next()
            self.norm_block(x3, bx, 0, A1, B1, hT, bh)
            P.dma("sp", hTv[:, :, t0:t0 + 512], hT, reads=[bh], pwrites=[bs["hT"]])
            qo, bqo = qor.next()
            ko, bko = kor.next()
            for (w, bw, oT, boT, scl) in ((wq, bwq, qo, bqo, 1.0), (wk, bwk, ko, bko, 1.0 / 16.0)):
                for h in range(H):
                    pss = []
                    for half in range(2):
                        oc = 2 * h + half
                        ps, bps = self.next_ps()
                        pss.append((ps, bps))
                        for kc in range(8):
                            P.op("pe", lambda e, ps=ps, w=w, kc=kc, oc=oc, hT=hT: e.matmul(
                                ps[:, :], lhsT=w[:, kc, oc * 128:(oc + 1) * 128], rhs=hT[:, kc, :],
                                start=(kc == 0), stop=(kc == 7)), [bw[h], bh], pwrites=[bps])
                    xx, bxx = x12.next()
                    for half in range(2):
                        P.op("act", lambda e, xx=xx, half=half, ps=pss[half][0], scl=scl: e.activation(
                            out=xx[:, half, :], in_=ps[:, :], func=AF.Identity, scale=scl), [pss[half][1]], pwrites=[bxx])
                    t4, bt4 = tq.next()
                    for ti, (xi, ci) in enumerate(((0, 0), (1, 1), (0, 1), (1, 0))):
                        P.op("dve", lambda e, t4=t4, ti=ti, xx=xx, xi=xi, ci=ci, rt=rt: e.tensor_tensor(
                            out=t4[:, ti, :], in0=xx[:, xi, :], in1=rt[:, ci, :], op=ALU.mult), [bxx, brt], pwrites=[bt4])
                    P.op("pool", lambda e, t4=t4, oT=oT, h=h: e.tensor_tensor(
                        out=oT[:, 2 * h, :], in0=t4[:, 0, :], in1=t4[:, 1, :], op=ALU.subtract), [bt4], pwrites=[boT])
                    P.op("pool", lambda e, t4=t4, oT=oT, h=h: e.tensor_tensor(
                        out=oT[:, 2 * h + 1, :], in0=t4[:, 2, :], in1=t4[:, 3, :], op=ALU.add), [bt4], pwrites=[boT])
            P.dma("sp", qTv[:, :, t0:t0 + 512], qo, reads=[bqo], pwrites=[bs["qT"]])
            P.dma("sp", kTv[:, :, t0:t0 + 512], ko, reads=[bko], pwrites=[bs["kT"]])
            kt, bkt = ktr.next()
            for s4 in range(4):
                ps, bps = self.next_ps()
                psT = ps[:, :].bitcast(BF16)
                for c in range(8):
                    P.op("pe", lambda e, psT=psT, ko=ko, c=c, s4=s4: e.transpose(
                        psT[:, c * 128:(c + 1) * 128], ko[:, c, s4 * 128:(s4 + 1) * 128], self.ident),
                        [bko, self.bpb], pwrites=[bps])
                P.op("act", lambda e, psT=psT, kt=kt, s4=s4: e.activation(out=kt[:, s4, :], in_=psT[:, 0:1024], func=AF.Identity),
                     [bps], pwrites=[bkt])
            P.dma("sp", ktv[:, tt * 4:(tt + 1) * 4, :], kt, reads=[bkt], pwrites=[bs["ktok"]])
        self.phase_barrier()
        wv, bwv = self.alloc([8, 2048], BF16), [Buf("wv%d" % i) for i in range(4)]
        wg, bwg = self.alloc([8, 2048], BF16), [Buf("wg%d" % i) for i in range(4)]
        for blk in range(4):
            P.dma("pool", wv[:, :, blk * 512:(blk + 1) * 512], win[:, :, 2048 + blk * 512:2048 + (blk + 1) * 512],
                  writes=[bwv[blk]])
            P.dma("pool", wg[:, :, blk * 512:(blk + 1) * 512], win[:, :, 4096 + blk * 512:4096 + (blk + 1) * 512],
                  writes=[bwg[blk]])
        hring = self.ring(2, [8, 512], BF16, "hT")
        vor = self.ring(2, [4, 2048], BF16, "vo")
        gor = self.ring(2, [4, 2048], BF16, "go")
        flip = 0
        def r1b_load(tt):
            hT, bh = hring.next()
            P.dma("sp", hT, hTv[:, :, tt * 512:(tt + 1) * 512], reads=[bs["hT"]], writes=[bh])
            return hT, bh

        nxt = r1b_load(0)
        for tt in range(8):
            t0 = tt * 512
            hT, bh = nxt
            if tt + 1 < 8:
                nxt = r1b_load(tt + 1)
            vo, bvo = vor.next()
            go, bgo = gor.next()
            for s4 in range(4):
                for isg, (w, bw, o_, bo_) in enumerate(((wv, bwv, vo, bvo), (wg, bwg, go, bgo))):
                    for blk in range(4):
                        ps, bps = self.next_ps()
                        for kc in range(8):
                            P.op("pe", lambda e, ps=ps, hT=hT, w=w, kc=kc, s4=s4, blk=blk: e.matmul(
                                ps[:, :], lhsT=hT[:, kc, s4 * 128:(s4 + 1) * 128], rhs=w[:, kc, blk * 512:(blk + 1) * 512],
                                start=(kc == 0), stop=(kc == 7)), [bh, bw[blk]], pwrites=[bps])
                        if isg:
                            P.op("act", lambda e, ps=ps, o_=o_, s4=s4, blk=blk: e.activation(
                                out=o_[:, s4, blk * 512:(blk + 1) * 512], in_=ps[:, :], func=AF.Silu), [bps], pwrites=[bo_])
                        else:
                            P.op("dve", lambda e, ps=ps, o_=o_, s4=s4, blk=blk: e.tensor_copy(
                                out=o_[:, s4, blk * 512:(blk + 1) * 512], in_=ps[:, :]), [bps], pwrites=[bo_])
            P.dma("sp", vv[:, tt * 4:(tt + 1) * 4, :], vo, reads=[bvo], pwrites=[bs["v"]])
            P.dma("sp", sgv[:, tt * 4:(tt + 1) * 4, :], go, reads=[bgo], pwrites=[bs["sg"]])
        self.phase_barrier()
        self.ret_core(qTv, kTv, ktv, vv, sgv, zTv)
        self.phase_barrier()
        wo, bwo = self.alloc([16, 1024], BF16), [Buf("wo%d" % i) for i in range(4)]
        wsrc = self.w_ret_out[j].rearrange("(k p) n -> p k n", p=128)
        for og in range(4):
            P.dma("pool", wo[:, :, og * 256:(og + 1) * 256], wsrc[:, :, og * 256:(og + 1) * 256], writes=[bwo[og]])
        xring = self.ring(2, [8, 512], F32, "x")
        zring = self.ring(2, [16, 512], BF16, "zT")
        def r3_load(tt):
            x3, bx = xring.next()
            P.dma("sp", x3, xv[:, :, tt * 512:(tt + 1) * 512], reads=[self.bX[tt]], writes=[bx])
            zt, bzt = zring.next()
            P.dma("sp", zt, zTv[:, :, tt * 512:(tt + 1) * 512], reads=[bs["zT"]], writes=[bzt])
            return x3, bx, zt, bzt

        nxt = r3_load(0)
        for tt in range(8):
            t0 = tt * 512
            x3, bx, zt, bzt = nxt
            if tt + 1 < 8:
                nxt = r3_load(tt + 1)
            for oc in range(8):
                ps, bps = self.next_ps()
                for k in range(16):
                    P.op("pe", lambda e, ps=ps, k=k, oc=oc, zt=zt: e.matmul(
                        ps[:, :], lhsT=wo[:, k, oc * 128:(oc + 1) * 128], rhs=zt[:, k, :],
                        start=(k == 0), stop=(k == 15)), [bwo[oc // 2], bzt], pwrites=[bps])
                P.op("dve", lambda e, ps=ps, oc=oc, x3=x3: e.scalar_tensor_tensor(
                    out=x3[:, oc, :], in0=ps[:, :], scalar=G1[:, oc:oc + 1], in1=x3[:, oc, :],
                    op0=ALU.mult, op1=ALU.add), [bps, self.bmod, bx], pwrites=[bx])
            P.dma("sp", xv[:, :, t0:t0 + 512], x3, reads=[bx], pwrites=[self.bX[tt]])

    def ret_core(self, qTv, kTv, ktv, vv, sgv, zTv):
        P = self.P
        pf = self.pf
        C = _consts()
        bs = self.bscr
        GC = 2
        NG = NCH // GC
        qgr = self.ring(2, [8, GC * 128], BF16, "qg")
        kgr = self.ring(2, [8, GC * 128], BF16, "kg")
        ktgr = self.ring(2, [GC, 1024], BF16, "ktg")
        vgr = self.ring(2, [GC, 2048], BF16, "vg")
        sggr = self.ring(2, [GC, 2048], BF16, "sgg")
        St = [(self.alloc([2, 512], F32), Buf("St%d" % h)) for h in range(H)]
        sbr = self.ring(8, [2, 512], BF16, "sb")
        rfbr = [self.ring(2, [2, 512], BF16, "rfb%d" % h) for h in range(H)]
        zTr = self.ring(2, [16, 512], BF16, "zTs")
        ptr = self.ring(4, [128], BF16, "pt")
        qfr = self.ring(4, [2, 128], BF16, "qf")
        qbr = self.ring(4, [2, 128], BF16, "qb")
        kdr = self.ring(4, [256], BF16, "kd")
        ynr = self.ring(4, [512], F32, "yn")
        zr = self.ring(8, [512], BF16, "z")
        str_ = self.ring(8, [16], F32, "st")
        MT = [pf[:, OMT + h * 128:OMT + (h + 1) * 128] for h in range(H)]
        DFb = [pf[:, OQD + h * 128:OQD + (h + 1) * 128].unsqueeze(1).to_broadcast([128, 2, 128]) for h in range(H)]
        DBb = [pf[:, OQD + 512 + h * 128:OQD + 512 + (h + 1) * 128].unsqueeze(1).to_broadcast([128, 2, 128])
               for h in range(H)]
        KF = [pf[:, OKD + h:OKD + h + 1] for h in range(H)]
        KB = [pf[:, OKD + 4 + h:OKD + 5 + h] for h in range(H)]
        for h in range(H):
            P.op("dve", lambda e, h=h: e.memset(St[h][0].rearrange("p a b -> p (a b)"), 0.0), [], [St[h][1]])
        def s1_load(g):
            ktg, bktg = ktgr.next()
            vg, bvg = vgr.next()
            P.dma("sp", ktg, ktv[:, g * GC:(g + 1) * GC, :], reads=[bs["ktok"]], writes=[bktg])
            P.dma("sp", vg, vv[:, g * GC:(g + 1) * GC, :], reads=[bs["v"]], writes=[bvg])
            return ktg, bktg, vg, bvg

        nxt = s1_load(NG - 1)
        for g in range(NG - 1, -1, -1):
            ktg, bktg, vg, bvg = nxt
            if g > 0:
                nxt = s1_load(g - 1)
            for c in range(GC - 1, -1, -1):
                n = g * GC + c
                if n == 0:
                    continue
                kbs = []
                for h in range(H):
                    kb, bkb = kdr.next()
                    kbs.append((kb, bkb))
                    P.op("dve", lambda e, kb=kb, h=h, c=c, ktg=ktg: e.tensor_scalar(
                        out=kb, in0=ktg[:, c, h * 256:(h + 1) * 256], scalar1=KB[h], scalar2=None, op0=ALU.mult),
                        [bktg, self.bpf], [bkb])
                pss = []
                for h in range(H):
                    kb, bkb = kbs[h]
                    for dc in range(2):
                        ps, bps = self.next_ps()
                        pss.append((ps, bps))
                        P.op("pe", lambda e, ps=ps, kb=kb, dc=dc, c=c, h=h, vg=vg: e.matmul(
                            ps[:, :], lhsT=kb[:, dc * 128:(dc + 1) * 128], rhs=vg[:, c, h * 512:(h + 1) * 512],
                            start=True, stop=True), [bkb, bvg], [bps])
                for h in range(H):
                    Sh, bSh = St[h]
                    for dc in range(2):
                        ps, bps = pss[h * 2 + dc]
                        P.op("dve", lambda e, ps=ps, dc=dc, Sh=Sh, cd=C["cdb"][h]: e.scalar_tensor_tensor(
                            out=Sh[:, dc, :], in0=Sh[:, dc, :], scalar=cd, in1=ps[:, :], op0=ALU.mult, op1=ALU.add),
                            [bps, bSh], pwrites=[bSh])
                for h in range(H):
                    Sh, bSh = St[h]
                    sbb, bsbb = sbr.next()
                    P.op("act", lambda e, sbb=sbb, Sh=Sh: e.activation(out=sbb, in_=Sh, func=AF.Identity), [bSh], [bsbb])
                    P.dma("sp", self.RBd[h, n].rearrange("(dc p) v -> p dc v", p=128), sbb, reads=[bsbb],
                          pwrites=[bs["RB"]])
        for h in range(H):
            P.op("dve", lambda e, h=h: e.memset(St[h][0].rearrange("p a b -> p (a b)"), 0.0), [], [St[h][1]])
        rfb = [None] * H
        pend = []
        self.ucnt = 0
        zTs, bz = None, None
        def s2_load(g):
            qg, bqg = qgr.next()
            kg, bkg = kgr.next()
            ktg, bktg = ktgr.next()
            vg, bvg = vgr.next()
            sgg, bsgg = sggr.next()
            tsl = slice(g * GC * 128, (g + 1) * GC * 128)
            P.dma("sp", qg, qTv[:, :, tsl], reads=[bs["qT"]], writes=[bqg])
            P.dma("sp", kg, kTv[:, :, tsl], reads=[bs["kT"]], writes=[bkg])
            P.dma("sp", ktg, ktv[:, g * GC:(g + 1) * GC, :], reads=[bs["ktok"]], writes=[bktg])
            P.dma("sp", vg, vv[:, g * GC:(g + 1) * GC, :], reads=[bs["v"]], writes=[bvg])
            P.dma("sp", sgg, sgv[:, g * GC:(g + 1) * GC, :], reads=[bs["sg"]], writes=[bsgg])
            return qg, bqg, kg, bkg, ktg, bktg, vg, bvg, sgg, bsgg

        def sbn_load(n):
            r = [None] * H
            if n < NCH - 1:
                for h in range(H):
                    t_, b_ = sbr.next()
                    r[h] = (t_, b_)
                    P.dma("sp", t_, self.RBd[h, n + 1].rearrange("(dc p) v -> p dc v", p=128),
                          reads=[bs["RB"]], writes=[b_])
            return r

        nxtg = s2_load(0)
        nxts = sbn_load(0)
        for g in range(NG):
            qg, bqg, kg, bkg, ktg, bktg, vg, bvg, sgg, bsgg = nxtg
            if g + 1 < NG:
                nxtg = s2_load(g + 1)
            for c in range(GC):
                n = g * GC + c
                if n % 4 == 0:
                    zTs, bz = zTr.next()
                sl = slice(c * 128, (c + 1) * 128)
                first, last = (n == 0), (n == NCH - 1)
                sbn = nxts
                if n + 1 < NCH:
                    nxts = sbn_load(n + 1)
                psS, bpS = self.ps[0], self.psb[0]
                for h in range(H):
                    for dc in range(2):
                        P.op("pe", lambda e, h=h, dc=dc, sl=sl, kg=kg, qg=qg: e.matmul(
                            psS[:, h * 128:(h + 1) * 128], lhsT=kg[:, 2 * h + dc, sl], rhs=qg[:, 2 * h + dc, sl],
                            start=(dc == 0), stop=(dc == 1)), [bkg, bqg], pwrites=[bpS])
                PTs, qfs, qbs, kfs = [], [], [], []
                for h in range(H):
                    PT, bPT = ptr.next()
                    PTs.append((PT, bPT))
                    P.op("dve", lambda e, PT=PT, h=h: e.tensor_tensor(
                        out=PT, in0=psS[:, h * 128:(h + 1) * 128], in1=MT[h], op=ALU.mult), [bpS, self.bpf], [bPT])
                for h in range(H):
                    if not first:
                        qf, bqf = qfr.next()
                        qfs.append((qf, bqf))
                        P.op("pool", lambda e, qf=qf, h=h, sl=sl, qg=qg: e.tensor_tensor(
                            out=qf, in0=qg[:, 2 * h:2 * h + 2, sl], in1=DFb[h], op=ALU.mult), [bqg, self.bpf], [bqf])
                    if not last:
                        qb, bqb = qbr.next()
                        qbs.append((qb, bqb))
                        P.op("pool", lambda e, qb=qb, h=h, sl=sl, qg=qg: e.tensor_tensor(
                            out=qb, in0=qg[:, 2 * h:2 * h + 2, sl], in1=DBb[h], op=ALU.mult), [bqg, self.bpf], [bqb])
                        kf, bkf = kdr.next()
                        kfs.append((kf, bkf))
                        P.op("act", lambda e, kf=kf, h=h, c=c, ktg=ktg: e.activation(
                            out=kf, in_=ktg[:, c, h * 256:(h + 1) * 256], func=AF.Identity, scale=KF[h]),
                            [bktg, self.bpf], [bkf])
                for pd in pend:
                    self._ret_transpose(pd)
                pend = []
                prev = None
                for h in range(H):
                    PT, bPT = PTs[h]
                    mm = [(PT, vg[:, c, h * 512:(h + 1) * 512], [bPT, bvg])]
                    if not last:
                        qb, bqb = qbs[h]
                        t_, b_ = sbn[h]
                        for dc in range(2):
                            mm.append((qb[:, dc, :], t_[:, dc, :], [bqb, b_]))
                    if not first:
                        qf, bqf = qfs[h]
                        r_, br_ = rfb[h]
                        for dc in range(2):
                            mm.append((qf[:, dc, :], r_[:, dc, :], [bqf, br_]))
                    psY, bpY = self.ps[1 + h], self.psb[1 + h]
                    for i, (l_, r2, rd_) in enumerate(mm):
                        P.op("pe", lambda e, psY=psY, l_=l_, r2=r2, i=i, lst=(i == len(mm) - 1): e.matmul(
                            psY[:, :], lhsT=l_, rhs=r2, start=(i == 0), stop=lst), rd_, pwrites=[bpY])
                    psU = []
                    if not last:
                        kf, bkf = kfs[h]
                        for dc in range(2):
                            ub = 5 + (self.ucnt % 3)
                            self.ucnt += 1
                            ps, bps = self.ps[ub], self.psb[ub]
                            psU.append((ps, bps))
                            P.op("pe", lambda e, ps=ps, kf=kf, dc=dc, c=c, h=h, vg=vg: e.matmul(
                                ps[:, :], lhsT=kf[:, dc * 128:(dc + 1) * 128], rhs=vg[:, c, h * 512:(h + 1) * 512],
                                start=True, stop=True), [bkf, bvg], [bps])
                    st_, bst = str_.next()
                    P.op("dve", lambda e, st_=st_, psY=psY: e.bn_stats(out=st_[:, 0:6], in_=psY[:, :]), [bpY], [bst])
                    P.op("dve", lambda e, st_=st_: e.bn_aggr(out=st_[:, 8:10], in_=st_[:, 0:6]), [bst], [bst])
                    P.op("act", lambda e, st_=st_: e.activation(out=st_[:, 10:11], in_=st_[:, 9:10], func=AF.Sqrt,
                                                               bias=self.eps, scale=1.0), [bst, self.bpf], [bst])
                    if prev is not None:
                        self._ret_gate_act(prev)
                    if not last:
                        Sh, bSh = St[h]
                        for dc in range(2):
                            ps, bps = psU[dc]
                            P.op("dve", lambda e, ps=ps, dc=dc, Sh=Sh, cd=C["cdf"][h]: e.scalar_tensor_tensor(
                                out=Sh[:, dc, :], in0=Sh[:, dc, :], scalar=cd, in1=ps[:, :], op0=ALU.mult, op1=ALU.add),
                                [bps, bSh], pwrites=[bSh])
                        r_, br_ = rfbr[h].next()
                        rfb[h] = (r_, br_)
                        P.op("act", lambda e, r_=r_, Sh=Sh: e.activation(out=r_, in_=Sh, func=AF.Identity), [bSh], [br_])
                    if prev is not None:
                        self._ret_gate_dve(prev)
                    P.op("dve", lambda e, st_=st_: e.reciprocal(out=st_[:, 11:12], in_=st_[:, 10:11]), [bst], [bst])
                    P.op("dve", lambda e, st_=st_: e.scalar_tensor_tensor(
                        out=st_[:, 12:13], in0=st_[:, 8:9], scalar=-1.0, in1=st_[:, 11:12], op0=ALU.mult, op1=ALU.mult),
                        [bst], [bst])
                    z, bzz = zr.next()
                    yn, byn = ynr.next()
                    prev = (st_, bst, psY, bpY, yn, byn, z, bzz, sgg, bsgg, c, h)
                    pend.append((z, bzz, n, h, zTs, bz, 1 + h))
                self._ret_gate_act(prev)
                self._ret_gate_dve(prev)
                if n % 4 == 3:
                    for pd in pend:
                        self._ret_transpose(pd)
                    pend = []
                    t0 = (n - 3) * 128
                    P.dma("sp", zTv[:, :, t0:t0 + 512], zTs, reads=[bz], pwrites=[bs["zT"]])

    def _ret_gate_act(self, a):
        P = self.P
        st_, bst, psY, bpY, yn, byn, z, bzz, sgg, bsgg, c, h = a
        P.op("act", lambda e: e.activation(
            out=yn, in_=psY[:, :], func=AF.Identity, bias=st_[:, 12:13], scale=st_[:, 11:12]), [bpY, bst], [byn])

    def _ret_gate_dve(self, a):
        P = self.P
        st_, bst, psY, bpY, yn, byn, z, bzz, sgg, bsgg, c, h = a
        P.op("dve", lambda e: e.tensor_tensor(
            out=z, in0=yn, in1=sgg[:, c, h * 512:(h + 1) * 512], op=ALU.mult), [byn, bsgg], [bzz])

    def _ret_transpose(self, pd):
        P = self.P
        z, bzz, n, h, zTs, bz, bank = pd
        ps, bps = self.ps[bank], self.psb[bank]
        psT = ps[:, :].bitcast(BF16)
        for vc in range(4):
            P.op("pe", lambda e, psT=psT, z=z, vc=vc: e.transpose(
                psT[:, vc * 128:(vc + 1) * 128], z[:, vc * 128:(vc + 1) * 128], self.ident),
                [bzz, self.bpb], pwrites=[bps])
        tl = (n % 4) * 128
        P.op("act", lambda e, psT=psT, tl=tl, h=h, zTs=zTs: e.activation(
            out=zTs[:, 4 * h:4 * h + 4, tl:tl + 128], in_=psT[:, 0:512].rearrange("p (a b) -> p a b", a=4),
            func=AF.Identity), [bps], pwrites=[bz])

    def final_norm(self):
        P = self.P
        Af, Bf = self.modcol(NL, "AF"), self.modcol(NL, "BF")
        xv = (self.xres if self.nsub > 0 else self.xin).rearrange("(kc p) t -> p kc t", p=128)
        ov = self.out.rearrange("(kc p) t -> p kc t", p=128)
        self.norm_setup()
        xring = self.ring(2, [8, 512], F32, "x")
        oring = self.ring(2, [8, 512], F32, "o")
        def fin_load(tt):
            x3, bx = xring.next()
            P.dma("sp", x3, xv[:, :, tt * 512:(tt + 1) * 512], reads=[self.bX[tt]], writes=[bx])
            return x3, bx

        nxt = fin_load(0)
        for tt in range(8):
            t0 = tt * 512
            x3, bx = nxt
            if tt + 1 < 8:
                nxt = fin_load(tt + 1)
            o3, bo = oring.next()
            self.norm_block(x3, bx, 0, Af, Bf, o3, bo)
            P.dma("sp", ov[:, :, t0:t0 + 512], o3, reads=[bo], pwrites=[self.bOut])


_NC_CACHE = {}


def _col(v):
    return np.ascontiguousarray(np.asarray(v, dtype=np.float32).reshape(-1, 128).T)


def kernel(x, c, w_ada, b_ada, norm_mix_g, norm_ffn_g, w_fourier_out, w_ret_in, w_ret_out,
           w_ffn_in, w_ffn_out, final_norm_g, w_ada_final, b_ada_final, _nsub=8, _final=True, _cores=None):
    C = _consts()
    key = (_nsub, _final)
    if key not in _NC_CACHE:
        _NC_CACHE[key] = Builder(_nsub, _final).build()
    nc = _NC_CACHE[key]
    f32 = lambda a: np.ascontiguousarray(np.asarray(a, dtype=np.float32))
    shared = {
        "smallb": C["smallb"], "twtab": C["twtab"], "rot": C["rot"],
        "w_ada": f32(w_ada), "w_ada_final": f32(w_ada_final), "w_fourier_out": f32(w_fourier_out),
        "w_ret_in": f32(w_ret_in), "w_ret_out": f32(w_ret_out), "w_ffn_in": f32(w_ffn_in),
        "w_ffn_out": f32(w_ffn_out),
    }
    cores = list(range(8)) if _cores is None else _cores
    in_maps = []
    for b in cores:
        sf = C["sf"].copy()
        sf[:, OC:OC + 8] = _col(c[b])
        for li in range(NL):
            sf[:, OGM + li * 8:OGM + li * 8 + 8] = _col(norm_mix_g[li])
            sf[:, OGF + li * 8:OGF + li * 8 + 8] = _col(norm_ffn_g[li])
            sf[:, OBA + li * 48:OBA + (li + 1) * 48] = _col(b_ada[li])
        sf[:, OGN:OGN + 8] = _col(final_norm_g)
        sf[:, OBF:OBF + 16] = _col(b_ada_final)
        m = dict(shared)
        m["smallf"] = sf
        m["xT"] = np.ascontiguousarray(np.asarray(x[b], dtype=np.float32).T)
        in_maps.append(m)
    res = run_bass_kernel_spmd(nc, in_maps, core_ids=list(range(len(cores))))
    import os
    if os.environ.get("KDBGZ"):
        return res.results[0]
    out = np.empty((len(cores), S, D), dtype=np.float32)
    for i in range(len(cores)):
        out[i] = res.results[i]["outT"].T
    return out
```
